# Optimizing a Trainium2 kernel written in Bass

```python
import jax, jax.numpy as jnp
from jax import lax
import numpy as np

D_MODEL = 1024
BATCH = 2
SEQ = 16384
DEPTH = 2
DEC_BATCH = 32
DEC_SEQ = 2048
PAST_LEN = 128

N_EVEN = (DEPTH + 1) // 2
N_ODD = DEPTH // 2

CONV_CH = D_MODEL // 2
CONV_WIDTH = 31
ATT_HEADS = 8
ATT_HEAD_DIM = 64
ATT_WIDTH = ATT_HEADS * ATT_HEAD_DIM
DILATED_PATTERNS = ((128, 1), (512, 4), (2048, 16))
N_PATTERNS = len(DILATED_PATTERNS)
ROPE_THETA = 500000.0
ROPE_DIM = ATT_HEAD_DIM // 4
AB_IN = 2 * CONV_CH + N_PATTERNS * 3 * ATT_WIDTH
AB_OUT = CONV_CH + ATT_WIDTH
MLSTM_HEADS = 8
MLSTM_WIDTH = D_MODEL
MLSTM_HEAD_DIM = MLSTM_WIDTH // MLSTM_HEADS
MLSTM_CHUNK = 128
C_IN = 4 * MLSTM_WIDTH + 4 * MLSTM_HEADS
D_FF = 4 * D_MODEL

EPS = 1e-6
NEG_INF = -1e30

kernel_name = "hybrid_conv_dilated_mlstm_encoder"


def rms_norm(x, g):
    xf = x.astype(jnp.float32)
    y = xf * lax.rsqrt(jnp.mean(xf * xf, axis=-1, keepdims=True) + EPS)
    return (y * g.astype(jnp.float32)).astype(x.dtype)


def layer_norm(x, g, b):
    xf = x.astype(jnp.float32)
    mu = jnp.mean(xf, axis=-1, keepdims=True)
    xc = xf - mu
    y = xc * lax.rsqrt(jnp.mean(xc * xc, axis=-1, keepdims=True) + EPS)
    return (y * g.astype(jnp.float32) + b.astype(jnp.float32)).astype(x.dtype)


def partial_rope(x, positions):
    half = ROPE_DIM // 2
    inv_freq = jnp.power(ROPE_THETA, -jnp.arange(half, dtype=jnp.float32) / half)
    ang = positions.astype(jnp.float32)[:, None] * inv_freq[None, :]
    cos = jnp.cos(ang)[None, :, None, :]
    sin = jnp.sin(ang)[None, :, None, :]
    x1 = x[..., :half]
    x2 = x[..., half:ROPE_DIM]
    rot = jnp.concatenate([x1 * cos - x2 * sin, x2 * cos + x1 * sin], axis=-1)
    return jnp.concatenate([rot, x[..., ROPE_DIM:]], axis=-1)


def dilated_window_attention(q, k, v, window, dilation):
    N, S, H, hd = q.shape
    radius = window // (2 * dilation)
    L = S // dilation
    blk = radius
    nb = -(-L // blk)
    Lp = nb * blk
    Nd = N * dilation

    def to_sub(t):
        return t.reshape(N, L, dilation, H, hd).transpose(0, 2, 1, 3, 4).reshape(Nd, L, H, hd)

    qs, ks, vs = to_sub(q), to_sub(k), to_sub(v)
    qb = jnp.pad(qs, ((0, 0), (0, Lp - L), (0, 0), (0, 0))).reshape(Nd, nb, blk, H, hd)

    def neighbours(t):
        tp = jnp.pad(t, ((0, 0), (blk, Lp - L + blk), (0, 0), (0, 0))).reshape(Nd, nb + 2, blk, H, hd)
        return jnp.concatenate([tp[:, :-2], tp[:, 1:-1], tp[:, 2:]], axis=2)

    kw, vw = neighbours(ks), neighbours(vs)
    s = jnp.einsum('nbqhd,nbkhd->nbhqk', qb, kw) * (hd ** -0.5)
    qpos = jnp.arange(nb)[:, None] * blk + jnp.arange(blk)[None, :]
    kpos = (jnp.arange(nb)[:, None] - 1) * blk + jnp.arange(3 * blk)[None, :]
    kp = kpos[:, None, :]
    mask = (jnp.abs(qpos[:, :, None] - kp) <= radius) & (kp >= 0) & (kp < L)
    s = jnp.where(mask[None, :, None], s, NEG_INF)
    m = jnp.max(s, axis=-1, keepdims=True)
    p = jnp.exp(s - m)
    l = jnp.sum(p, axis=-1, keepdims=True)
    o = jnp.einsum('nbhqk,nbkhd->nbqhd', p, vw) / l.transpose(0, 1, 3, 2, 4)
    lse = (m + jnp.log(l))[..., 0].transpose(0, 1, 3, 2)
    o = o.reshape(Nd, Lp, H, hd)[:, :L]
    lse = lse.reshape(Nd, Lp, H)[:, :L]
    o = o.reshape(N, dilation, L, H, hd).transpose(0, 2, 1, 3, 4).reshape(N, S, H, hd)
    lse = lse.reshape(N, dilation, L, H).transpose(0, 2, 1, 3).reshape(N, S, H)
    return o, lse


def conv_attn_mixer(h, w_in, conv_w, conv_b, ln_g, ln_b, w_out):
    N, S, _ = h.shape
    proj = h @ w_in
    a_lin = proj[..., :CONV_CH]
    a_gate = proj[..., CONV_CH:2 * CONV_CH]
    qkv = proj[..., 2 * CONV_CH:]
    a = a_lin * jax.nn.sigmoid(a_gate)
    a = lax.conv_general_dilated(
        a, conv_w[:, None, :].astype(a.dtype), window_strides=(1,),
        padding=[(CONV_WIDTH // 2, CONV_WIDTH // 2)],
        dimension_numbers=('NWC', 'WIO', 'NWC'), feature_group_count=CONV_CH) + conv_b
    a = jax.nn.silu(layer_norm(a, ln_g, ln_b))
    qkv = qkv.reshape(N, S, N_PATTERNS, 3, ATT_HEADS, ATT_HEAD_DIM).astype(jnp.float32)
    pos = jnp.arange(S)
    outs, lses = [], []
    for g, (window, dilation) in enumerate(DILATED_PATTERNS):
        q = partial_rope(qkv[:, :, g, 0], pos)
        k = partial_rope(qkv[:, :, g, 1], pos)
        o, lse = dilated_window_attention(q, k, qkv[:, :, g, 2], window, dilation)
        outs.append(o)
        lses.append(lse)
    wts = jax.nn.softmax(jnp.stack(lses, axis=0), axis=0)
    att = jnp.einsum('pnsh,pnshd->nshd', wts, jnp.stack(outs, axis=0))
    att = att.reshape(N, S, ATT_WIDTH).astype(h.dtype)
    return jnp.concatenate([a.astype(h.dtype), att], axis=-1) @ w_out


def mlstm_chunk_scan(q, k, v, ig, lf):
    N, H, S, d = q.shape
    L = MLSTM_CHUNK
    nc = S // L

    def chunks(t):
        return jnp.moveaxis(t.reshape(N, H, nc, L, *t.shape[3:]), 2, 0)

    causal = jnp.tril(jnp.ones((L, L), dtype=bool))

    def step(carry, xs):
        C, n, m = carry
        qc, kc, vc, igc, lfc = xs
        b = jnp.cumsum(lfc, axis=-1)
        logD = jnp.where(causal, b[..., :, None] - b[..., None, :] + igc[..., None, :], NEG_INF)
        inter = b + m[..., None]
        mt = jnp.maximum(inter, jnp.max(logD, axis=-1))
        sc = jnp.einsum('nhtd,nhsd->nhts', qc, kc) * jnp.exp(logD - mt[..., None])
        ei = jnp.exp(inter - mt)
        num = jnp.einsum('nhts,nhse->nhte', sc, vc) + ei[..., None] * jnp.einsum('nhtd,nhde->nhte', qc, C)
        den = jnp.sum(sc, axis=-1) + ei * jnp.einsum('nhtd,nhd->nht', qc, n)
        hc = num / jnp.maximum(jnp.abs(den), jnp.exp(-mt))[..., None]
        bL = b[..., -1]
        wlog = bL[..., None] - b + igc
        m_new = jnp.maximum(bL + m, jnp.max(wlog, axis=-1))
        decay = jnp.exp(bL + m - m_new)
        w = jnp.exp(wlog - m_new[..., None])
        kw = kc * w[..., None]
        C_new = decay[..., None, None] * C + jnp.einsum('nhsd,nhse->nhde', kw, vc)
        n_new = decay[..., None] * n + jnp.sum(kw, axis=2)
        return (C_new, n_new, m_new), hc

    init = (jnp.zeros((N, H, d, d), jnp.float32), jnp.zeros((N, H, d), jnp.float32),
            jnp.zeros((N, H), jnp.float32))
    _, hs = lax.scan(step, init, (chunks(q), chunks(k), chunks(v), chunks(ig), chunks(lf)))
    return jnp.moveaxis(hs, 0, 2).reshape(N, H, S, d)


def mlstm_mixer(h, w_in, gate_b, head_norm, w_out):
    N, S, _ = h.shape
    W, H, dh = MLSTM_WIDTH, MLSTM_HEADS, MLSTM_HEAD_DIM
    proj = (h @ w_in).astype(jnp.float32)

    def heads(t):
        return t.reshape(N, S, H, dh).transpose(0, 2, 1, 3)

    q = heads(proj[..., :W])
    k = heads(proj[..., W:2 * W]) * (dh ** -0.5)
    v = heads(proj[..., 2 * W:3 * W])
    o = proj[..., 3 * W:4 * W]
    gates = (proj[..., 4 * W:] + gate_b.astype(jnp.float32)).reshape(N, S, 4, H).transpose(2, 0, 3, 1)
    ig_f, f_f, ig_b, f_b = gates[0], gates[1], gates[2], gates[3]

    def flip(t):
        return jnp.flip(t, axis=2)

    qq = jnp.concatenate([q, flip(q)], axis=0)
    kk = jnp.concatenate([k, flip(k)], axis=0)
    vv = jnp.concatenate([v, flip(v)], axis=0)
    ig = jnp.concatenate([ig_f, flip(ig_b)], axis=0)
    lf = jnp.concatenate([jax.nn.log_sigmoid(f_f), flip(jax.nn.log_sigmoid(f_b))], axis=0)
    hh = mlstm_chunk_scan(qq, kk, vv, ig, lf)
    ht = hh[:N] + flip(hh[N:])
    ht = ht * lax.rsqrt(jnp.mean(ht * ht, axis=-1, keepdims=True) + EPS)
    ht = ht.transpose(0, 2, 1, 3).reshape(N, S, W) * head_norm.astype(jnp.float32)
    return (jax.nn.sigmoid(o) * ht).astype(h.dtype) @ w_out


def squared_relu_mlp(h, w1, w2):
    return jnp.square(jax.nn.relu(h @ w1)) @ w2


def trunk(x, c, ada_w, ada_b, norm_mix, norm_mlp, ab_w_in, conv_w, conv_b, conv_ln_g,
          conv_ln_b, ab_w_out, c_w_in, c_gate_b, c_head_norm, c_w_out, mlp_w1, mlp_w2,
          norm_final):
    for layer in range(DEPTH):
        mod = jax.nn.silu(c.astype(jnp.float32)) @ ada_w[layer].astype(jnp.float32) + ada_b[layer].astype(jnp.float32)
        mod = mod.astype(x.dtype)[:, None, :]
        sh1, sc1, g1, sh2, sc2, g2 = jnp.split(mod, 6, axis=-1)
        h = rms_norm(x, norm_mix[layer]) * (1 + sc1) + sh1
        i = layer // 2
        if layer % 2 == 0:
            mix = conv_attn_mixer(h, ab_w_in[i], conv_w[i], conv_b[i], conv_ln_g[i],
                                  conv_ln_b[i], ab_w_out[i])
        else:
            mix = mlstm_mixer(h, c_w_in[i], c_gate_b[i], c_head_norm[i], c_w_out[i])
        x = x + g1 * mix
        h = rms_norm(x, norm_mlp[layer]) * (1 + sc2) + sh2
        x = x + g2 * squared_relu_mlp(h, mlp_w1[layer], mlp_w2[layer])
    return rms_norm(x, norm_final)


def setup_inputs(seed: int = 0) -> dict:
    key = jax.random.key(seed)
    ks = jax.random.split(key, 26)

    def nrm(k, shape, scale):
        return jax.random.normal(k, shape, jnp.float32) * scale

    D = D_MODEL
    c_main = nrm(ks[14], (N_ODD, D, 4 * MLSTM_WIDTH), D ** -0.5)
    c_gates = nrm(ks[15], (N_ODD, D, 4 * MLSTM_HEADS), 0.1 * D ** -0.5)
    ig_bias = nrm(ks[16], (N_ODD, 2, MLSTM_HEADS), 0.1)
    f_bias = 3.0 + 3.0 * jax.random.uniform(ks[17], (N_ODD, 2, MLSTM_HEADS), jnp.float32)
    gate_b = jnp.stack([ig_bias[:, 0], f_bias[:, 0], ig_bias[:, 1], f_bias[:, 1]], axis=1)
    return {
        "x_prompt": nrm(ks[0], (BATCH, SEQ, D), 1.0),
        "x_sample": nrm(ks[1], (DEC_BATCH, DEC_SEQ, D), 1.0),
        "c_prompt": nrm(ks[2], (BATCH, D), 1.0),
        "c_sample": nrm(ks[3], (DEC_BATCH, D), 1.0),
        "ada_w": nrm(ks[4], (DEPTH, D, 6 * D), D ** -0.5),
        "ada_b": nrm(ks[5], (DEPTH, 6 * D), 0.02),
        "norm_mix": 1.0 + nrm(ks[6], (DEPTH, D), 0.05),
        "norm_mlp": 1.0 + nrm(ks[7], (DEPTH, D), 0.05),
        "ab_w_in": nrm(ks[8], (N_EVEN, D, AB_IN), D ** -0.5),
        "conv_w": nrm(ks[9], (N_EVEN, CONV_WIDTH, CONV_CH), CONV_WIDTH ** -0.5),
        "conv_b": nrm(ks[10], (N_EVEN, CONV_CH), 0.02),
        "conv_ln_g": 1.0 + nrm(ks[11], (N_EVEN, CONV_CH), 0.05),
        "conv_ln_b": nrm(ks[12], (N_EVEN, CONV_CH), 0.02),
        "ab_w_out": nrm(ks[13], (N_EVEN, AB_OUT, D), AB_OUT ** -0.5),
        "c_w_in": jnp.concatenate([c_main, c_gates], axis=-1),
        "c_gate_b": gate_b.reshape(N_ODD, 4 * MLSTM_HEADS),
        "c_head_norm": 1.0 + nrm(ks[18], (N_ODD, MLSTM_WIDTH), 0.05),
        "c_w_out": nrm(ks[19], (N_ODD, MLSTM_WIDTH, D), MLSTM_WIDTH ** -0.5),
        "mlp_w1": nrm(ks[20], (DEPTH, D, D_FF), D ** -0.5),
        "mlp_w2": nrm(ks[21], (DEPTH, D_FF, D), D_FF ** -0.5),
        "norm_final": 1.0 + nrm(ks[22], (D,), 0.05),
    }


def reference(x_prompt, x_sample, c_prompt, c_sample, ada_w, ada_b, norm_mix, norm_mlp,
              ab_w_in, conv_w, conv_b, conv_ln_g, conv_ln_b, ab_w_out, c_w_in, c_gate_b,
              c_head_norm, c_w_out, mlp_w1, mlp_w2, norm_final):
    y_prompt = trunk(x_prompt, c_prompt, ada_w, ada_b, norm_mix, norm_mlp, ab_w_in, conv_w,
                     conv_b, conv_ln_g, conv_ln_b, ab_w_out, c_w_in, c_gate_b, c_head_norm,
                     c_w_out, mlp_w1, mlp_w2, norm_final)
    y_sample = trunk(x_sample, c_sample, ada_w, ada_b, norm_mix, norm_mlp, ab_w_in, conv_w,
                     conv_b, conv_ln_g, conv_ln_b, ab_w_out, c_w_in, c_gate_b, c_head_norm,
                     c_w_out, mlp_w1, mlp_w2, norm_final)
    return (y_prompt, y_sample)
```

```python
import numpy as np
import concourse.bass as bass
import concourse.mybir as mybir
from concourse.bass_utils import run_bass_kernel_spmd

F32 = mybir.dt.float32
BF16 = mybir.dt.bfloat16
AF = mybir.ActivationFunctionType
ALU = mybir.AluOpType

SEG = 2048
D = 1024
TT = 512
PADK = 1024
PADC = 16
EPS = 1e-6
NEG = -30000.0
PATTERNS = ((128, 1), (512, 4), (2048, 16))

QUEUES = ("sp", "pe", "act", "dve", "pool")
SEM_CAP = 30000
DMA_RING = 12


class Op:
    __slots__ = ("q", "fn", "deps", "signal", "idx", "sig_epoch", "sig_val",
                 "dma", "dma_i")

    def __init__(self, q, fn, dma):
        self.q = q
        self.fn = fn
        self.dma = dma
        self.deps = set()
        self.signal = False
        self.sig_epoch = 0
        self.sig_val = 0
        self.dma_i = -1


class Sched:
    def __init__(self, nc):
        self.nc = nc
        self.ops = []
        self.last_w = {}
        self.readers = {}
        self.pending = {q: None for q in QUEUES}
        self.last_c = {q: None for q in QUEUES}
        self.last_d = {q: [] for q in QUEUES}

    def barrier(self):
        b = set()
        for q in QUEUES:
            if self.last_c[q] is not None:
                b.add(self.last_c[q])
            b.update(self.last_d[q][-DMA_RING:])
        for q in QUEUES:
            self.pending[q] = set(b) | (self.pending[q] or set())
        self.last_w = {}
        self.readers = {}

    def add(self, q, fn, reads=(), writes=(), dma=False):
        op = Op(q, fn, dma)
        op.idx = len(self.ops)
        deps = set()
        for k in reads:
            w = self.last_w.get(k)
            if w is not None:
                deps.add(w)
        for k in writes:
            w = self.last_w.get(k)
            if w is not None:
                deps.add(w)
            for r in self.readers.get(k, ()):
                deps.add(r)
        if self.pending[q] is not None:
            deps |= self.pending[q]
            self.pending[q] = None
        for d in deps:
            dop = self.ops[d]
            if dop.q == "pe" and q == "pe" and not dop.dma and not dma:
                continue
            op.deps.add(d)
        for k in writes:
            self.last_w[k] = op.idx
            self.readers[k] = []
        for k in reads:
            if k in writes:
                continue
            self.readers.setdefault(k, []).append(op.idx)
        if dma:
            self.last_d[q].append(op.idx)
        else:
            self.last_c[q] = op.idx
        self.ops.append(op)
        return op

    def emit(self):
        nc = self.nc
        ops = self.ops
        for op in ops:
            for d in op.deps:
                ops[d].signal = True
        cnt = {q: 0 for q in QUEUES}
        dcnt = {q: 0 for q in QUEUES}
        n_epochs = {q: 0 for q in QUEUES}
        for op in ops:
            if op.dma:
                op.dma_i = dcnt[op.q]
                dcnt[op.q] += 1
            elif op.signal:
                c = cnt[op.q]
                op.sig_epoch = c // SEM_CAP
                op.sig_val = c % SEM_CAP + 1
                cnt[op.q] = c + 1
                n_epochs[op.q] = op.sig_epoch + 1
        csems = {q: [nc.alloc_semaphore(f"c_{q}_{i}") for i in range(n_epochs[q])]
                 for q in QUEUES}
        dsems = {q: [nc.alloc_semaphore(f"d_{q}_{i}")
                     for i in range(min(DMA_RING, dcnt[q]))] for q in QUEUES}
        per_q = {q: [] for q in QUEUES}
        for op in ops:
            per_q[op.q].append(op)

        def target(dop):
            if not dop.dma:
                return (csems[dop.q][dop.sig_epoch], dop.sig_val,
                        ("c", dop.q, dop.sig_epoch))
            i = dop.dma_i
            return (dsems[dop.q][i % DMA_RING], 16 * (i // DMA_RING + 1),
                    ("d", dop.q, i % DMA_RING))

        def run_queue(qname, eng):
            waited = {}
            for op in per_q[qname]:
                need = {}
                for d in op.deps:
                    s, v, key = target(ops[d])
                    if waited.get(key, 0) >= v:
                        continue
                    if key not in need or need[key][1] < v:
                        need[key] = (s, v)
                if op.dma and op.dma_i >= DMA_RING:
                    i = op.dma_i - DMA_RING
                    key = ("d", qname, i % DMA_RING)
                    v = 16 * (i // DMA_RING + 1)
                    if waited.get(key, 0) < v and (key not in need or need[key][1] < v):
                        need[key] = (dsems[qname][i % DMA_RING], v)
                for key, (s, v) in need.items():
                    eng.wait_ge(s, v)
                    waited[key] = v
                ins = op.fn(eng)
                if op.dma:
                    ins.then_inc(dsems[qname][op.dma_i % DMA_RING], 16)
                elif op.signal:
                    ins.then_inc(csems[qname][op.sig_epoch], 1)
            n = dcnt[qname]
            for r in range(min(DMA_RING, n)):
                last_i = ((n - 1 - r) // DMA_RING) * DMA_RING + r
                v = 16 * (last_i // DMA_RING + 1)
                key = ("d", qname, r)
                if waited.get(key, 0) < v:
                    eng.wait_ge(dsems[qname][r], v)

        with nc.Block() as block:
            @block.sync
            def _(e):
                run_queue("sp", e)

            @block.tensor
            def _(e):
                run_queue("pe", e)

            @block.scalar
            def _(e):
                run_queue("act", e)

            @block.vector
            def _(e):
                run_queue("dve", e)

            @block.gpsimd
            def _(e):
                run_queue("pool", e)
        self.stats = {q: len(per_q[q]) for q in QUEUES}


class Arena:
    def __init__(self, nc, words):
        self.t = nc.alloc_sbuf_tensor("arena", [128, words], F32)
        self.words = words
        self.base = 0
        self.off = 0
        self.gen = 0

    def freeze(self):
        self.base = self.off

    def reset(self):
        self.off = self.base
        self.gen += 1

    def alloc(self, name, shape, dtype):
        n = 1
        for s in shape[1:]:
            n *= s
        w = n if dtype == F32 else (n + 1) // 2
        w = (w + 7) // 8 * 8
        assert self.off + w <= self.words, f"arena overflow at {name}: {self.off + w} > {self.words}"
        v = self.t[:, self.off:self.off + w]
        self.off += w
        if dtype != F32:
            v = v.bitcast(dtype)
        v = v[:, 0:n]
        if len(shape) == 3:
            v = v.rearrange("p (a b) -> p a b", a=shape[1])
        elif len(shape) == 4:
            v = v.rearrange("p (a b c) -> p a b c", a=shape[1], b=shape[2])
        return v


class Cx:
    pass


def _emitters(cx):
    S = cx.S

    def MM(out, lhsT, rhs, start, stop, R, W):
        S.add("pe", lambda e: e.matmul(out, lhsT=lhsT, rhs=rhs, start=start, stop=stop), R, W)

    def TR(out, in_, ident, R, W):
        S.add("pe", lambda e: e.transpose(out, in_, ident), R, W)

    def ACT(out, in_, func, R, W, scale=1.0, bias=0.0):
        S.add("act", lambda e: e.activation(out=out, in_=in_, func=func, bias=bias, scale=scale), R, W)

    def TTOP(q, out, in0, in1, op, R, W):
        S.add(q, lambda e: e.tensor_tensor(out=out, in0=in0, in1=in1, op=op), R, W)

    def STT(q, out, in0, scalar, in1, op0, op1, R, W):
        S.add(q, lambda e: e.scalar_tensor_tensor(out=out, in0=in0, scalar=scalar, in1=in1,
                                                  op0=op0, op1=op1), R, W)

    def TS(q, out, in0, s1, s2, op0, op1, R, W):
        if s2 is None:
            S.add(q, lambda e: e.tensor_scalar(out=out, in0=in0, scalar1=s1, scalar2=None, op0=op0), R, W)
        else:
            S.add(q, lambda e: e.tensor_scalar(out=out, in0=in0, scalar1=s1, scalar2=s2,
                                               op0=op0, op1=op1), R, W)

    def CP(q, out, in_, R, W):
        if q == "act":
            S.add("act", lambda e: e.copy(out=out, in_=in_), R, W)
        else:
            S.add(q, lambda e: e.tensor_copy(out=out, in_=in_), R, W)

    def RECIP(out, in_, R, W):
        S.add("dve", lambda e: e.reciprocal(out=out, in_=in_), R, W)

    def MEMSET(q, out, val, W):
        S.add(q, lambda e: e.memset(out, val), (), W)

    def DMA(q, out, in_, R, W):
        S.add(q, lambda e: e.dma_start(out=out, in_=in_), R, W, dma=True)

    cx.MM, cx.TR, cx.ACT, cx.TTOP, cx.STT, cx.TS, cx.CP = MM, TR, ACT, TTOP, STT, TS, CP
    cx.RECIP, cx.MEMSET, cx.DMA = RECIP, MEMSET, DMA


def dsl(start, d, n=128):
    return slice(start, start + d * (n - 1) + 1, d)


class Rot:
    def __init__(self, items):
        self.items = list(items)
        self.i = 0

    def __call__(self):
        v = self.items[self.i % len(self.items)]
        self.i += 1
        return v


def load_weight(cx, dst, src, kcs, ncols, key, coff=0, stage_words=2048, st=None):
    A = cx.A
    own = st is None
    if st is None:
        st = [A.alloc(f"wst{i}_{A.off}", [128, stage_words], F32) for i in range(2)]
    engs = Rot(["dve", "pool", "act"])
    i = 0
    stkey = key if own else "stg"
    for kc in range(kcs):
        for c0 in range(0, ncols, stage_words):
            n = min(stage_words, ncols - c0)
            s = st[i % 2]
            sk = f"{stkey}_st{i % 2}"
            cx.DMA("sp", s[:, 0:n], src[kc * 128:(kc + 1) * 128, coff + c0:coff + c0 + n], [], [sk])
            cx.CP(engs(), dst[:, kc, c0:c0 + n], s[:, 0:n], [sk], [key])
            i += 1


def rms_to_hT(cx, xT, hT, G, SH, seg, sq, tmp, rstd, psb, n=512, nch=8, inv=1.0 / 1024):
    pb = cx.ps[psb]
    for c in range(nch):
        s = sq[c % 2]
        cx.ACT(s[:, 0:n], xT[:, c, 0:n], AF.Square, [("xT", c)], [f"sq{c % 2}"])
        cx.MM(pb[:, 0:n], cx.ones_b, s[:, 0:n], c == 0, c == nch - 1, [f"sq{c % 2}"], [f"ps{psb}"])
    cx.ACT(rstd[:, 0:n], pb[:, 0:n], AF.Sqrt, [f"ps{psb}"], ["rstd"], scale=inv, bias=cx.eps_col)
    cx.RECIP(rstd[:, 0:n], rstd[:, 0:n], ["rstd"], ["rstd"])
    for c in range(nch):
        t = tmp[c % 2]
        cx.STT("dve", t[:, 0:n], xT[:, c, 0:n], G[:, c, seg:seg + 1], rstd[:, 0:n], ALU.mult, ALU.mult,
               [("xT", c), "rstd"], [f"tmpn{c % 2}"])
        cx.ACT(hT[:, c, 0:n], t[:, 0:n], AF.Identity, [f"tmpn{c % 2}"], [("hT", c)],
               scale=1.0, bias=SH[:, c, seg:seg + 1])


class _Stop(Exception):
    pass


def build(NSEG, debug=False, upto=99, sub=0):
    nc, cx = _build_head(NSEG, debug)
    cx.upto = upto
    cx.sub = sub
    cx.phase_no = 0
    try:
        _build_body(nc, cx, NSEG)
    except _Stop:
        pass
    cx.S.emit()
    cx.stats = cx.S.stats
    return nc, cx


def _build_head(NSEG, debug):
    nc = bass.Bass("TRN2", target_bir_lowering=False)
    cx = Cx()
    cx.nc = nc
    cx.S = Sched(nc)
    cx.debug = debug
    _emitters(cx)
    return nc, cx


def _build_body(nc, cx, NSEG):
    debug = cx.debug
    T = NSEG * SEG
    NT = T // TT
    S = cx.S
    _orig_barrier = S.barrier

    def gated_barrier():
        cx.phase_no += 1
        if cx.phase_no > cx.upto:
            raise _Stop()
        _orig_barrier()
    S.barrier = gated_barrier

    def ckpt(n):
        if cx.sub == n:
            raise _Stop()

    def din(name, shape):
        return nc.dram_tensor(name, list(shape), F32, kind="ExternalInput").ap()

    def dscr(name, shape, dt):
        skind = "ExternalOutput" if (debug and name in debug) else "Internal"
        return nc.dram_tensor(name, list(shape), dt, kind=skind).ap()

    x = din("x", [T, D])
    cT = din("cT", [128, 8, NSEG])
    ada_w = din("ada_w", [2, D, 6 * D])
    ada_bT = din("ada_bT", [128, 2, 48])
    norms = din("norms", [128, 5, 8])
    ab_w_in = din("ab_w_in", [D, 5632])
    conv_wT = din("conv_wT", [128, 4, 31])
    conv_vec = din("conv_vec", [128, 3, 4])
    ab_w_out = din("ab_w_out", [D, D])
    c_w_in = din("c_w_in", [D, 4128])
    gate_b = din("gate_b", [128, 32])
    head_norm = din("head_norm", [128, 8])
    c_w_out = din("c_w_out", [D, D])
    mlp_w1 = din("mlp_w1", [2, D, 4 * D])
    mlp_w2 = din("mlp_w2", [2, 4 * D, D])
    ropec = din("ropec", [128, T])
    ropes = din("ropes", [128, T])
    flags = din("flags", [128, 4, NSEG])
    NCONST = 128 * 8 + 256 * 2
    consts = din("consts", [128, NCONST])
    y = nc.dram_tensor("y", [T, D], F32, kind="ExternalOutput").ap()

    XT = dscr("XT", [8, 128, T], F32)
    AGLU = dscr("AGLU", [4, 128, T + 2 * PADC], F32)
    QKs = dscr("QKs", [3, 2, 4, 128, T + 2 * PADK], BF16)
    Vs = dscr("Vs", [3, T + 2 * PADK, 512], BF16)
    ACV = dscr("ACV", [4, 128, T], BF16)
    ATT = dscr("ATT", [4, 128, T], BF16)
    QT1 = dscr("QT1", [8, 128, T], BF16)
    KT1 = dscr("KT1", [8, 128, T], BF16)
    OS1 = dscr("OS1", [8, 128, T], BF16)
    VT1 = dscr("VT1", [T, D], BF16)
    GS1 = dscr("GS1", [T, 32], F32)
    HF = dscr("HF", [8, 128, T], F32)
    GT = dscr("GT", [8, 128, T], BF16)
    cx.dbg = dict(XT=XT, AGLU=AGLU, QKs=QKs, Vs=Vs, ACV=ACV, ATT=ATT, QT1=QT1, KT1=KT1, OS1=OS1,
                  VT1=VT1, GS1=GS1, HF=HF, GT=GT)

    A = Arena(nc, 53000)
    cx.A = A
    cx.ps = [nc.alloc_psum_tensor(f"ps{i}", [128, 512], F32)[:] for i in range(8)]
    ps = cx.ps
    MM, TR, ACT, TTOP, STT, TS, CP, RECIP, MEMSET, DMA = (cx.MM, cx.TR, cx.ACT, cx.TTOP, cx.STT,
                                                          cx.TS, cx.CP, cx.RECIP, cx.MEMSET, cx.DMA)

    cf = A.alloc("cf", [128, NCONST], F32)
    DMA("sp", cf, consts, [], ["cf"])
    o = 0
    identf = cf[:, o:o + 128]; o += 128
    Pm_f = cf[:, o:o + 128]; o += 128
    Utri = cf[:, o:o + 128]; o += 128
    Ltri = cf[:, o:o + 128]; o += 128
    maskU_f = cf[:, o:o + 128]; o += 128
    maskL_f = cf[:, o:o + 128]; o += 128
    onesf = cf[:, o:o + 128]; o += 128
    zerosf = cf[:, o:o + 128]; o += 128
    maskA_f = cf[:, o:o + 256]; o += 256
    maskB_f = cf[:, o:o + 256]; o += 256
    cb = A.alloc("cb", [128, 128 * 5 + 512], BF16)
    CP("dve", cb[:, 0:128], identf, ["cf"], ["cb"])
    CP("dve", cb[:, 128:256], Pm_f, ["cf"], ["cb"])
    CP("dve", cb[:, 256:384], maskU_f, ["cf"], ["cb"])
    CP("dve", cb[:, 384:512], maskL_f, ["cf"], ["cb"])
    CP("dve", cb[:, 512:640], onesf, ["cf"], ["cb"])
    CP("dve", cb[:, 640:896], maskA_f, ["cf"], ["cb"])
    CP("dve", cb[:, 896:1152], maskB_f, ["cf"], ["cb"])
    ident_b = cb[:, 0:128]
    Pm_b = cb[:, 128:256]
    maskU_b = cb[:, 256:384]
    maskL_b = cb[:, 384:512]
    cx.ones_b = ones_b = cb[:, 512:640]
    maskA_b = cb[:, 640:896]
    maskB_b = cb[:, 896:1152]
    epsc = A.alloc("epsc", [128, 8], F32)
    MEMSET("dve", epsc, EPS, ["epsc"])
    cx.eps_col = epsc[:, 0:1]
    fl = A.alloc("fl", [128, 4, NSEG], F32)
    DMA("sp", fl, flags, [], ["fl"])
    nrm = A.alloc("nrm", [128, 5, 8], F32)
    DMA("sp", nrm, norms, [], ["nrm"])
    cwt = A.alloc("cwt", [128, 4, 31], F32)
    DMA("sp", cwt, conv_wT, [], ["cwt"])
    cvec = A.alloc("cvec", [128, 3, 4], F32)
    DMA("sp", cvec, conv_vec, [], ["cvec"])
    gb = A.alloc("gb", [128, 32], F32)
    DMA("sp", gb, gate_b, [], ["gb"])
    hn = A.alloc("hn", [128, 8], F32)
    DMA("sp", hn, head_norm, [], ["hn"])
    abT = A.alloc("abT", [128, 2, 48], F32)
    DMA("sp", abT, ada_bT, [], ["abT"])
    modT = A.alloc("modT", [128, 2, 48, NSEG], F32)
    G1 = [A.alloc(f"G1_{l}", [128, 8, NSEG], F32) for l in range(2)]
    G2 = [A.alloc(f"G2_{l}", [128, 8, NSEG], F32) for l in range(2)]
    A.freeze()

    A.reset()
    cs = A.alloc("cs", [128, 8, NSEG], F32)
    DMA("sp", cs, cT, [], ["cs"])
    ACT(cs, cs, AF.Silu, ["cs"], ["cs"])
    wada = [A.alloc(f"wada{i}", [128, 8, 1024], F32) for i in range(2)]
    it = 0
    for l in range(2):
        for fb in range(6):
            wt = wada[it % 2]
            wk = f"wada{it % 2}"
            it += 1
            for kc in range(8):
                DMA("sp", wt[:, kc, :], ada_w[l, kc * 128:(kc + 1) * 128, fb * 1024:(fb + 1) * 1024], [], [wk])
            pb = fb % 2
            for f in range(8):
                for kc in range(8):
                    MM(ps[pb][:, f * NSEG:(f + 1) * NSEG], wt[:, kc, f * 128:(f + 1) * 128], cs[:, kc, :],
                       kc == 0, kc == 7, [wk, "cs"], [f"ps{pb}"])
            for f in range(8):
                fi = fb * 8 + f
                ACT(modT[:, l, fi, :], ps[pb][:, f * NSEG:(f + 1) * NSEG], AF.Identity, [f"ps{pb}", "abT"],
                    ["modT"], scale=1.0, bias=abT[:, l, fi:fi + 1])
    for l in range(2):
        for c in range(8):
            TS("dve", G1[l][:, c, :], modT[:, l, 8 + c, :], 1.0, nrm[:, l, c:c + 1], ALU.add, ALU.mult,
               ["modT", "nrm"], ["G"])
            TS("dve", G2[l][:, c, :], modT[:, l, 32 + c, :], 1.0, nrm[:, 2 + l, c:c + 1], ALU.add, ALU.mult,
               ["modT", "nrm"], ["G"])
    SH1 = [modT[:, l, 0:8, :] for l in range(2)]
    GG1 = [modT[:, l, 16:24, :] for l in range(2)]
    SH2 = [modT[:, l, 24:32, :] for l in range(2)]
    GG2 = [modT[:, l, 40:48, :] for l in range(2)]

    zt = A.alloc("zt", [128, 4, PADC], F32)
    MEMSET("pool", zt, 0.0, ["zt"])
    zb = A.alloc("zb", [128, 4, PADK], BF16)
    MEMSET("pool", zb, 0.0, ["zb"])
    zb8 = zb.rearrange("p a b -> p (a b)").rearrange("p (a c) -> p a c", c=512)
    DMA("sp", AGLU[:, :, 0:PADC].rearrange("c p t -> p c t"), zt, ["zt"], ["AGLU"])
    DMA("sp", AGLU[:, :, PADC + T:PADC + T + PADC].rearrange("c p t -> p c t"), zt, ["zt"], ["AGLU"])
    for p in range(3):
        for part in range(2):
            DMA("sp", QKs[p, part, :, :, 0:PADK].rearrange("c p t -> p c t"), zb, ["zb"], ["QKs"])
            DMA("sp", QKs[p, part, :, :, PADK + T:PADK + T + PADK].rearrange("c p t -> p c t"), zb, ["zb"], ["QKs"])
        for r0 in (0, PADK + T):
            DMA("sp", Vs[p, r0:r0 + PADK, :].rearrange("(a p) c -> p a c", p=128), zb8, ["zb"], ["Vs"])

    S.barrier()
    A.reset()
    W = A.alloc("w_in", [128, 8, 5632], BF16)
    load_weight(cx, W, ab_w_in, 8, 5632, "w_in")
    xs = A.alloc("xs", [128, 4, D], F32)
    xT = A.alloc("xT", [128, 8, TT], F32)
    sq = [A.alloc(f"sq{i}", [128, TT], BF16) for i in range(2)]
    rstd = A.alloc("rstd", [128, TT], F32)
    tmpn = [A.alloc(f"tmpn{i}", [128, TT], F32) for i in range(2)]
    hT = A.alloc("hT", [128, 8, TT], BF16)
    cosT = A.alloc("cosT", [128, TT], F32)
    sinT = A.alloc("sinT", [128, TT], F32)
    sig = [A.alloc(f"sig{i}", [128, TT], F32) for i in range(2)]
    ag = [A.alloc(f"ag{i}", [128, TT], F32) for i in range(2)]
    qb = [A.alloc(f"qb{i}", [128, TT], BF16) for i in range(2)]
    t1 = [A.alloc(f"t1{i}", [128, TT], F32) for i in range(2)]
    t2 = [A.alloc(f"t2{i}", [128, TT], F32) for i in range(2)]
    qo = [A.alloc(f"qo{i}", [128, TT], BF16) for i in range(2)]
    vst = A.alloc("vst", [128, 4, 1536], BF16)
    bank = Rot([0, 1, 2, 3])
    bank2 = Rot([4, 5])
    ev = Rot(["act", "dve"])
    for tt in range(NT):
        ckpt(10 + tt)
        seg = tt // 4
        t0 = tt * TT
        DMA("sp", xs, x[t0:t0 + TT, :].rearrange("(s p) d -> p s d", p=128), [], ["xs"])
        DMA("sp", cosT, ropec[:, t0:t0 + TT], [], ["cosT"])
        DMA("sp", sinT, ropes[:, t0:t0 + TT], [], ["sinT"])
        for c in range(8):
            b = bank()
            for s in range(4):
                TR(ps[b][:, s * 128:(s + 1) * 128], xs[:, s, c * 128:(c + 1) * 128], identf, ["xs", "cf"], [f"ps{b}"])
            CP(ev(), xT[:, c, :], ps[b], [f"ps{b}"], [("xT", c)])
        DMA("sp", XT[:, :, t0:t0 + TT].rearrange("c p t -> p c t"), xT, [("xT", c) for c in range(8)], ["XT"])
        ckpt(1)
        rms_to_hT(cx, xT, hT, G1[0], SH1[0], seg, sq, tmpn, rstd, 6)
        ckpt(2)
        hk = [("hT", c) for c in range(8)]
        for c in range(4):
            bl = bank()
            for kc in range(8):
                MM(ps[bl], W[:, kc, c * 128:(c + 1) * 128], hT[:, kc, :], kc == 0, kc == 7, ["w_in"] + hk, [f"ps{bl}"])
            bg = bank()
            for kc in range(8):
                MM(ps[bg], W[:, kc, 512 + c * 128:512 + (c + 1) * 128], hT[:, kc, :], kc == 0, kc == 7,
                   ["w_in"] + hk, [f"ps{bg}"])
            i = c % 2
            ACT(sig[i], ps[bg], AF.Sigmoid, [f"ps{bg}"], [f"sig{i}"])
            TTOP("dve", ag[i], ps[bl], sig[i], ALU.mult, [f"ps{bl}", f"sig{i}"], [f"ag{i}"])
            DMA("sp", AGLU[c, :, PADC + t0:PADC + t0 + TT], ag[i], [f"ag{i}"], ["AGLU"])
        ckpt(3)
        n = 0
        for p in range(3):
            for part in range(2):
                for hc in range(4):
                    col = 1024 + (p * 3 + part) * 512 + hc * 128
                    b = bank()
                    for kc in range(8):
                        MM(ps[b], W[:, kc, col:col + 128], hT[:, kc, :], kc == 0, kc == 7, ["w_in"] + hk, [f"ps{b}"])
                    i = n % 2
                    n += 1
                    ACT(qb[i], ps[b], AF.Identity, [f"ps{b}"], [f"qb{i}"], scale=(0.125 if part == 0 else 1.0))
                    b2 = bank2()
                    MM(ps[b2], Pm_b, qb[i], True, True, ["cb", f"qb{i}"], [f"ps{b2}"])
                    TTOP("dve", t1[i], qb[i], cosT, ALU.mult, [f"qb{i}", "cosT"], [f"t1{i}"])
                    TTOP("dve", t2[i], ps[b2], sinT, ALU.mult, [f"ps{b2}", "sinT"], [f"t2{i}"])
                    TTOP("pool", qo[i], t1[i], t2[i], ALU.add, [f"t1{i}", f"t2{i}"], [f"qo{i}"])
                    DMA("sp", QKs[p, part, hc, :, PADK + t0:PADK + t0 + TT], qo[i], [f"qo{i}"], ["QKs"])
        ckpt(4)
        for s in range(4):
            for p in range(3):
                col = 1024 + (p * 3 + 2) * 512
                b = bank()
                for kc in range(8):
                    MM(ps[b], hT[:, kc, s * 128:(s + 1) * 128], W[:, kc, col:col + 512], kc == 0, kc == 7,
                       ["w_in"] + hk, [f"ps{b}"])
                CP(ev(), vst[:, s, p * 512:(p + 1) * 512], ps[b], [f"ps{b}"], ["vst"])
        ckpt(5)
        for p in range(3):
            DMA("sp", Vs[p, PADK + t0:PADK + t0 + TT, :].rearrange("(s i) c -> i s c", i=128),
                vst[:, :, p * 512:(p + 1) * 512], ["vst"], ["Vs"])
        ckpt(6)

    S.barrier()
    A.reset()
    HW = TT + 30
    at = [A.alloc(f"at{i}", [128, 4, HW], F32) for i in range(2)]
    accD = [A.alloc(f"accD{i}", [128, TT], F32) for i in range(2)]
    accP = [A.alloc(f"accP{i}", [128, TT], F32) for i in range(2)]
    acc = A.alloc("acc", [128, 4, TT], F32)
    cab = [A.alloc(f"cab{i}", [128, TT], BF16) for i in range(2)]
    csq = [A.alloc(f"csq{i}", [128, TT], BF16) for i in range(2)]
    mean = A.alloc("mean", [128, TT], F32)
    var = A.alloc("var", [128, TT], F32)
    lt1 = [A.alloc(f"lt1{i}", [128, TT], F32) for i in range(2)]
    aco = A.alloc("aco", [128, 4, TT], BF16)
    NDVE = 31
    for tt in range(NT):
        seg, j = tt // 4, tt % 4
        t0 = tt * TT
        a_ = at[tt % 2]
        ak = f"at{tt % 2}"
        DMA("sp", a_, AGLU[:, :, PADC + t0 - 15:PADC + t0 + TT + 15].rearrange("c p t -> p c t"), ["AGLU"], [ak])
        if j == 0:
            TS("dve", a_[:, :, 0:15], a_[:, :, 0:15], fl[:, 0, seg:seg + 1], None, ALU.mult, None, [ak, "fl"], [ak])
        if j == 3:
            TS("dve", a_[:, :, TT + 15:TT + 30], a_[:, :, TT + 15:TT + 30], fl[:, 1, seg:seg + 1], None,
               ALU.mult, None, [ak, "fl"], [ak])
        for c in range(4):
            i = c % 2
            TS("dve", accD[i], a_[:, c, 0:TT], cwt[:, c, 0:1], cvec[:, 0, c:c + 1], ALU.mult, ALU.add,
               [ak, "cwt", "cvec"], [f"accD{i}"])
            for k in range(1, NDVE):
                STT("dve", accD[i], a_[:, c, k:k + TT], cwt[:, c, k:k + 1], accD[i], ALU.mult, ALU.add,
                    [ak, f"accD{i}"], [f"accD{i}"])
            CP("pool", acc[:, c, :], accD[i], [f"accD{i}"], [("acc", c)])
            ACT(cab[i], acc[:, c, :], AF.Identity, [("acc", c)], [f"cab{i}"])
            ACT(csq[i], acc[:, c, :], AF.Square, [("acc", c)], [f"csq{i}"])
            MM(ps[0], ones_b, cab[i], c == 0, c == 3, [f"cab{i}"], ["ps0"])
            MM(ps[1], ones_b, csq[i], c == 0, c == 3, [f"csq{i}"], ["ps1"])
        ACT(mean, ps[0], AF.Identity, ["ps0"], ["mean"], scale=1.0 / 512)
        TTOP("dve", var, mean, mean, ALU.mult, ["mean"], ["var"])
        STT("dve", var, ps[1], 1.0 / 512, var, ALU.mult, ALU.subtract, ["ps1", "var"], ["var"])
        ACT(var, var, AF.Sqrt, ["var"], ["var"], scale=1.0, bias=cx.eps_col)
        RECIP(var, var, ["var"], ["var"])
        for c in range(4):
            i = c % 2
            TTOP("dve", lt1[i], acc[:, c, :], mean, ALU.subtract, [("acc", c), "mean"], [f"lt1{i}"])
            TTOP("pool", lt1[i], lt1[i], var, ALU.mult, [f"lt1{i}", "var"], [f"lt1{i}"])
            ACT(aco[:, c, :], lt1[i], AF.Silu, [f"lt1{i}", "cvec"], ["aco"],
                scale=cvec[:, 1, c:c + 1], bias=cvec[:, 2, c:c + 1])
        DMA("sp", ACV[:, :, t0:t0 + TT].rearrange("c p t -> p c t"), aco, ["aco"], ["ACV"])

    S.barrier()
    A.reset()
    qt_ = [A.alloc(f"qt{i}", [128, SEG], BF16) for i in range(2)]
    kt_ = [A.alloc(f"kt{i}", [128, SEG + 128 * 16], BF16) for i in range(2)]
    vd = [A.alloc(f"vd{i}", [128, 32, 512], BF16) for i in range(2)]
    num_sb = A.alloc("num_sb", [128, 4, SEG], F32)
    den_sb = A.alloc("den_sb", [128, 4, SEG], F32)
    pt = [A.alloc(f"pt{i}", [128, 256], BF16) for i in range(4)]
    atto = [A.alloc(f"atto{i}", [128, SEG], BF16) for i in range(2)]
    sbank = Rot([0, 1, 2, 3])
    nbank = Rot([4, 5])
    dbank = Rot([6, 7])
    ptr = Rot([0, 1, 2, 3])
    ld = 0
    vl = 0
    for seg in range(NSEG):
        sb0 = seg * SEG
        for p, (win, d) in enumerate(PATTERNS):
            nqb = SEG // d // 128
            v_ = vd[vl % 2]
            vk = f"vd{vl % 2}"
            vl += 1
            for r in range(d):
                for m in range(nqb + 1):
                    start = PADK + sb0 - 64 * d + r + d * 128 * m
                    DMA("sp", v_[:, r * (nqb + 1) + m, :], Vs[p, dsl(start, d, 128), :], ["Vs"], [vk])
            for hc in range(4):
                q_ = qt_[ld % 2]
                k_ = kt_[ld % 2]
                qk, kk = f"qt{ld % 2}", f"kt{ld % 2}"
                ld += 1
                DMA("sp", q_, QKs[p, 0, hc, :, PADK + sb0:PADK + sb0 + SEG], ["QKs"], [qk])
                kw_ = SEG + 128 * d
                DMA("sp", k_[:, 0:kw_], QKs[p, 1, hc, :, PADK + sb0 - 64 * d:PADK + sb0 + SEG + 64 * d], ["QKs"], [kk])
                for r in range(d):
                    for j in range(nqb):
                        nb_, db_ = nbank(), dbank()
                        qa = q_[:, dsl(128 * d * j + r, d)]
                        for side in range(2):
                            m = j + side
                            ka = k_[:, dsl(128 * d * m + r, d)]
                            b = sbank()
                            pb = ps[b]
                            mk = maskA_b if side == 0 else maskB_b
                            MM(pb[:, 0:128], ident_b, mk[:, 0:128], True, False, ["cb"], [f"ps{b}"])
                            MM(pb[:, 0:128], ka[0:64, :], qa[0:64, :], False, True, [kk, qk], [f"ps{b}"])
                            MM(pb[:, 128:256], ident_b, mk[:, 0:128], True, False, ["cb"], [f"ps{b}"])
                            MM(pb[:, 128:256], ka[64:128, :], qa[64:128, :], False, True, [kk, qk], [f"ps{b}"])
                            pi = ptr()
                            if m == 0:
                                bias = fl[:, 2, seg:seg + 1]
                            elif m == nqb:
                                bias = fl[:, 3, seg:seg + 1]
                            else:
                                bias = 0.0
                            ACT(pt[pi], pb[:, 0:256], AF.Exp, [f"ps{b}", "fl"], [f"pt{pi}"], scale=1.0, bias=bias)
                            vt = v_[:, r * (nqb + 1) + m, hc * 128:(hc + 1) * 128]
                            MM(ps[nb_][0:64, 0:128], vt[:, 0:64], pt[pi][:, 0:128], side == 0, side == 1,
                               [vk, f"pt{pi}"], [f"ps{nb_}"])
                            MM(ps[nb_][64:128, 0:128], vt[:, 64:128], pt[pi][:, 128:256], side == 0, side == 1,
                               [vk, f"pt{pi}"], [f"ps{nb_}"])
                            MM(ps[db_][0:64, 0:128], ones_b[:, 0:64], pt[pi][:, 0:128], side == 0, side == 1,
                               ["cb", f"pt{pi}"], [f"ps{db_}"])
                            MM(ps[db_][64:128, 0:128], ones_b[:, 0:64], pt[pi][:, 128:256], side == 0, side == 1,
                               ["cb", f"pt{pi}"], [f"ps{db_}"])
                        na = num_sb[:, hc, dsl(128 * d * j + r, d)]
                        da = den_sb[:, hc, dsl(128 * d * j + r, d)]
                        if p == 0:
                            CP("dve", na, ps[nb_][:, 0:128], [f"ps{nb_}"], [("num", hc)])
                            CP("act", da, ps[db_][:, 0:128], [f"ps{db_}"], [("den", hc)])
                        else:
                            TTOP("dve", na, na, ps[nb_][:, 0:128], ALU.add, [f"ps{nb_}", ("num", hc)], [("num", hc)])
                            TTOP("dve", da, da, ps[db_][:, 0:128], ALU.add, [f"ps{db_}", ("den", hc)], [("den", hc)])
        for hc in range(4):
            ao = atto[hc % 2]
            RECIP(den_sb[:, hc, :], den_sb[:, hc, :], [("den", hc)], [("den", hc)])
            TTOP("pool", ao, num_sb[:, hc, :], den_sb[:, hc, :], ALU.mult, [("num", hc), ("den", hc)], [f"atto{hc % 2}"])
            DMA("sp", ATT[hc, :, sb0:sb0 + SEG], ao, [f"atto{hc % 2}"], ["ATT"])

    def phase_outproj(wdram, srcs, Gt):
        S.barrier()
        A.reset()
        Wo = A.alloc("w_o", [128, 8, D], BF16)
        load_weight(cx, Wo, wdram, 8, D, "w_o")
        inb = [A.alloc(f"inb{i}", [128, 8, TT], BF16) for i in range(2)]
        xr = [A.alloc(f"xr{i}", [128, 8, TT], F32) for i in range(2)]
        bk = Rot([0, 1, 2, 3])
        for tt in range(NT):
            seg = tt // 4
            t0 = tt * TT
            ib, xb_ = inb[tt % 2], xr[tt % 2]
            ik, xk = f"inb{tt % 2}", f"xr{tt % 2}"
            for (src, nchunk, c0, key) in srcs:
                DMA("sp", ib[:, c0:c0 + nchunk, :], src[:, :, t0:t0 + TT].rearrange("c p t -> p c t"), [key], [ik])
            DMA("sp", xb_, XT[:, :, t0:t0 + TT].rearrange("c p t -> p c t"), ["XT"], [xk])
            for m in range(8):
                b = bk()
                for kc in range(8):
                    MM(ps[b], Wo[:, kc, m * 128:(m + 1) * 128], ib[:, kc, :], kc == 0, kc == 7, ["w_o", ik], [f"ps{b}"])
                STT("dve", xb_[:, m, :], ps[b], Gt[:, m, seg:seg + 1], xb_[:, m, :], ALU.mult, ALU.add,
                    [f"ps{b}", xk, "modT"], [xk])
            DMA("sp", XT[:, :, t0:t0 + TT].rearrange("c p t -> p c t"), xb_, [xk], ["XT"])

    def phase_mlp(l, final):
        S.barrier()
        A.reset()
        stg = [A.alloc(f"stg{i}", [128, 2048], F32) for i in range(2)]
        W1 = A.alloc("w1", [128, 8, 4 * D], BF16)
        load_weight(cx, W1, mlp_w1[l], 8, 4 * D, "w1", st=stg)
        W2 = A.alloc("w2", [128, 32, D], BF16)
        load_weight(cx, W2, mlp_w2[l], 32, D, "w2", stage_words=1024, st=stg)
        xr = [A.alloc("xr0", [128, 8, TT], F32)] * 2
        sq_ = [A.alloc(f"sq{i}", [128, TT], BF16) for i in range(2)]
        rstd_ = A.alloc("rstd", [128, TT], F32)
        tmp_ = [A.alloc(f"tmpn{i}", [128, TT], F32) for i in range(2)]
        h2 = A.alloc("hT", [128, 8, TT], BF16)
        hid = A.alloc("hid", [128, 8, TT], BF16)
        rl = [A.alloc(f"rl{i}", [128, TT], F32) for i in range(2)]
        yo = [stg[i].rearrange("p (a b) -> p a b", a=2) for i in range(2)]
        stk = ["stg_st0", "stg_st1"]
        bk = Rot([0, 1, 2, 3])
        bk2 = Rot([4, 5])
        e2 = Rot(["dve", "act"])
        for tt in range(NT):
            seg = tt // 4
            t0 = tt * TT
            xb_ = xr[tt % 2]
            xk = "xr0"
            DMA("sp", xb_, XT[:, :, t0:t0 + TT].rearrange("c p t -> p c t"), ["XT"], [xk])
            for c in range(8):
                s = sq_[c % 2]
                ACT(s, xb_[:, c, :], AF.Square, [xk], [f"sq{c % 2}"])
                MM(ps[6], ones_b, s, c == 0, c == 7, [f"sq{c % 2}"], ["ps6"])
            ACT(rstd_, ps[6], AF.Sqrt, ["ps6"], ["rstd"], scale=1.0 / 1024, bias=cx.eps_col)
            RECIP(rstd_, rstd_, ["rstd"], ["rstd"])
            for c in range(8):
                t = tmp_[c % 2]
                STT("dve", t, xb_[:, c, :], G2[l][:, c, seg:seg + 1], rstd_, ALU.mult, ALU.mult,
                    [xk, "rstd", "G"], [f"tmpn{c % 2}"])
                ACT(h2[:, c, :], t, AF.Identity, [f"tmpn{c % 2}", "modT"], [("hT", c)], scale=1.0,
                    bias=SH2[l][:, c, seg:seg + 1])
            hk = [("hT", c) for c in range(8)]
            for qtr in range(4):
                for f in range(8):
                    fc = qtr * 8 + f
                    b = bk()
                    for kc in range(8):
                        MM(ps[b], W1[:, kc, fc * 128:(fc + 1) * 128], h2[:, kc, :], kc == 0, kc == 7,
                           ["w1"] + hk, [f"ps{b}"])
                    i = f % 2
                    if i == 0:
                        ACT(rl[i], ps[b], AF.Relu, [f"ps{b}"], [f"rl{i}"])
                        TTOP("pool", hid[:, f, :], rl[i], rl[i], ALU.mult, [f"rl{i}"], [("hid", f)])
                    else:
                        TS("dve", rl[i], ps[b], 0.0, None, ALU.max, None, [f"ps{b}"], [f"rl{i}"])
                        ACT(hid[:, f, :], rl[i], AF.Square, [f"rl{i}"], [("hid", f)])
                hdk = [("hid", f) for f in range(8)]
                for m in range(8):
                    b = bk2()
                    for f in range(8):
                        fc = qtr * 8 + f
                        MM(ps[b], W2[:, fc, m * 128:(m + 1) * 128], hid[:, f, :], f == 0, f == 7,
                           ["w2"] + hdk, [f"ps{b}"])
                    STT("dve", xb_[:, m, :], ps[b], GG2[l][:, m, seg:seg + 1], xb_[:, m, :], ALU.mult, ALU.add,
                        [f"ps{b}", xk, "modT"], [xk])
            if not final:
                DMA("sp", XT[:, :, t0:t0 + TT].rearrange("c p t -> p c t"), xb_, [xk], ["XT"])
            else:
                for c in range(8):
                    s = sq_[c % 2]
                    ACT(s, xb_[:, c, :], AF.Square, [xk], [f"sq{c % 2}"])
                    MM(ps[6], ones_b, s, c == 0, c == 7, [f"sq{c % 2}"], ["ps6"])
                ACT(rstd_, ps[6], AF.Sqrt, ["ps6"], ["rstd"], scale=1.0 / 1024, bias=cx.eps_col)
                RECIP(rstd_, rstd_, ["rstd"], ["rstd"])
                for c in range(8):
                    STT("dve", xb_[:, c, :], xb_[:, c, :], nrm[:, 4, c:c + 1], rstd_, ALU.mult, ALU.mult,
                        [xk, "rstd", "nrm"], [xk])
                for s in range(4):
                    yv = yo[s // 2][:, s % 2, :]
                    for cc in range(2):
                        b = bk()
                        for c4 in range(4):
                            c = cc * 4 + c4
                            TR(ps[b][:, c4 * 128:(c4 + 1) * 128], xb_[:, c, s * 128:(s + 1) * 128], identf,
                               [xk, "cf"], [f"ps{b}"])
                        CP(e2(), yv[:, cc * 512:(cc + 1) * 512], ps[b], [f"ps{b}"], stk)
                for hy in range(2):
                    DMA("sp", y[t0 + hy * 256:t0 + (hy + 1) * 256, :].rearrange("(s p) d -> p s d", p=128),
                        yo[hy], stk, ["y"])

    phase_outproj(ab_w_out, [(ACV, 4, 0, "ACV"), (ATT, 4, 4, "ATT")], GG1[0])
    phase_mlp(0, False)

    S.barrier()
    A.reset()
    Wc = A.alloc("w_c", [128, 8, 4128], BF16)
    load_weight(cx, Wc, c_w_in, 8, 4128, "w_c")
    xr = [A.alloc(f"xr{i}", [128, 8, TT], F32) for i in range(2)]
    sq = [A.alloc(f"sq{i}", [128, TT], BF16) for i in range(2)]
    rstd = A.alloc("rstd", [128, TT], F32)
    tmpn = [A.alloc(f"tmpn{i}", [128, TT], F32) for i in range(2)]
    hT = A.alloc("hT", [128, 8, TT], BF16)
    fo = [A.alloc(f"fo{i}", [128, 8, TT], BF16) for i in range(3)]
    vst1 = A.alloc("vst1", [128, 4, D], BF16)
    gst = A.alloc("gst", [128, 4, 32], F32)
    bank = Rot([0, 1, 2, 3])
    ev = Rot(["act", "dve"])
    KSC = 128.0 ** -0.5
    for tt in range(NT):
        seg = tt // 4
        t0 = tt * TT
        xb_ = xr[tt % 2]
        xk = f"xr{tt % 2}"
        DMA("sp", xb_, XT[:, :, t0:t0 + TT].rearrange("c p t -> p c t"), ["XT"], [xk])
        for c in range(8):
            s = sq[c % 2]
            ACT(s, xb_[:, c, :], AF.Square, [xk], [f"sq{c % 2}"])
            MM(ps[6], ones_b, s, c == 0, c == 7, [f"sq{c % 2}"], ["ps6"])
        ACT(rstd, ps[6], AF.Sqrt, ["ps6"], ["rstd"], scale=1.0 / 1024, bias=cx.eps_col)
        RECIP(rstd, rstd, ["rstd"], ["rstd"])
        for c in range(8):
            t = tmpn[c % 2]
            STT("dve", t, xb_[:, c, :], G1[1][:, c, seg:seg + 1], rstd, ALU.mult, ALU.mult,
                [xk, "rstd", "G"], [f"tmpn{c % 2}"])
            ACT(hT[:, c, :], t, AF.Identity, [f"tmpn{c % 2}", "modT"], [("hT", c)], scale=1.0,
                bias=SH1[1][:, c, seg:seg + 1])
        hk = [("hT", c) for c in range(8)]
        for which, cbase in ((0, 0), (1, 1024), (2, 3072)):
            for h in range(8):
                b = bank()
                col = cbase + h * 128
                for kc in range(8):
                    MM(ps[b], Wc[:, kc, col:col + 128], hT[:, kc, :], kc == 0, kc == 7, ["w_c"] + hk, [f"ps{b}"])
                if which == 0:
                    CP(ev(), fo[0][:, h, :], ps[b], [f"ps{b}"], ["fo0"])
                elif which == 1:
                    ACT(fo[1][:, h, :], ps[b], AF.Identity, [f"ps{b}"], ["fo1"], scale=KSC)
                else:
                    ACT(fo[2][:, h, :], ps[b], AF.Sigmoid, [f"ps{b}"], ["fo2"])
        DMA("sp", QT1[:, :, t0:t0 + TT].rearrange("c p t -> p c t"), fo[0], ["fo0"], ["QT1"])
        DMA("sp", KT1[:, :, t0:t0 + TT].rearrange("c p t -> p c t"), fo[1], ["fo1"], ["KT1"])
        DMA("sp", OS1[:, :, t0:t0 + TT].rearrange("c p t -> p c t"), fo[2], ["fo2"], ["OS1"])
        for s in range(4):
            for half in range(2):
                b = bank()
                col = 2048 + half * 512
                for kc in range(8):
                    MM(ps[b], hT[:, kc, s * 128:(s + 1) * 128], Wc[:, kc, col:col + 512], kc == 0, kc == 7,
                       ["w_c"] + hk, [f"ps{b}"])
                CP(ev(), vst1[:, s, half * 512:(half + 1) * 512], ps[b], [f"ps{b}"], ["vst1"])
            for kc in range(8):
                MM(ps[7][:, s * 32:(s + 1) * 32], hT[:, kc, s * 128:(s + 1) * 128], Wc[:, kc, 4096:4128],
                   kc == 0, kc == 7, ["w_c"] + hk, ["ps7"])
            TTOP("dve", gst[:, s, :], ps[7][:, s * 32:(s + 1) * 32], gb, ALU.add, ["ps7", "gb"], ["gst"])
        for g0 in (8, 24):
            ACT(gst[:, :, g0:g0 + 8], gst[:, :, g0:g0 + 8], AF.Exp, ["gst"], ["gst"], scale=-1.0)
            ACT(gst[:, :, g0:g0 + 8], gst[:, :, g0:g0 + 8], AF.Ln, ["gst"], ["gst"], scale=1.0, bias=1.0)
            TS("dve", gst[:, :, g0:g0 + 8], gst[:, :, g0:g0 + 8], -1.0, None, ALU.mult, None, ["gst"], ["gst"])
        DMA("sp", VT1[t0:t0 + TT, :].rearrange("(s i) c -> i s c", i=128), vst1, ["vst1"], ["VT1"])
        DMA("sp", GS1[t0:t0 + TT, :].rearrange("(s i) c -> i s c", i=128), gst, ["gst"], ["GS1"])

    NG = T // TT

    def phase_scan(bwd):
        S.barrier()
        A.reset()
        qg = [A.alloc(f"qg{i}", [128, 8, TT], BF16) for i in range(2)]
        kg = [A.alloc(f"kg{i}", [128, 8, TT], BF16) for i in range(2)]
        vg = [A.alloc(f"vg{i}", [128, 4, 8, 256], BF16) for i in range(2)]
        gg = [A.alloc(f"gg{i}", [128, 4, 32], F32) for i in range(2)]
        CN = A.alloc("CN", [128, 8, 256], F32)
        CNb = A.alloc("CNb", [128, 8, 256], BF16)
        hst = [A.alloc(f"hst{i}", [128, 8, TT], F32) for i in range(2)]
        asb = A.alloc("asb", [128, 8], F32)
        wsb = A.alloc("wsb", [128, 8], F32)
        dcy = A.alloc("dcy", [128, 8], F32)
        eb = [A.alloc(f"eb{i}", [128, 128], F32) for i in range(8)]
        dtl = [A.alloc(f"dt{i}", [128, 128], F32) for i in range(8)]
        scb = [A.alloc(f"scb{i}", [128, 128], BF16) for i in range(8)]
        qtb = [A.alloc(f"qtb{i}", [128, 128], BF16) for i in range(8)]
        kwb = [A.alloc(f"kwb{i}", [128, 128], BF16) for i in range(8)]
        rr = [A.alloc(f"rr{i}", [128, 128], F32) for i in range(8)]
        if bwd:
            hfg = [A.alloc("hfg0", [128, 8, TT], F32)] * 2
            osg = [A.alloc("osg0", [128, 8, TT], BF16)] * 2
            sqb = A.alloc("sqb", [128, TT], BF16)
            rs = A.alloc("rs", [128, TT], F32)
            gto = [A.alloc("gto0", [128, 8, TT], BF16)] * 2
        MEMSET("dve", CN, 0.0, ["CN"])
        MEMSET("pool", CNb, 0.0, ["CNb"])
        for i in range(2):
            MEMSET("pool", vg[i][:, :, :, 128:256], 1.0, [f"vg{i}"])
        Tri = Ltri if bwd else Utri
        mskb = maskL_f if bwd else maskU_f
        igo, lfo = (16, 24) if bwd else (0, 8)
        psT = ps[7].bitcast(BF16)
        for gi in range(NG):
            g = NG - 1 - gi if bwd else gi
            t0 = g * TT
            bi = gi % 2
            q_, k_, v_, g_, hs_ = qg[bi], kg[bi], vg[bi], gg[bi], hst[bi]
            qk_, kk_, vk_, gk_, hk_ = f"qg{bi}", f"kg{bi}", f"vg{bi}", f"gg{bi}", f"hst{bi}"
            DMA("sp", q_, QT1[:, :, t0:t0 + TT].rearrange("c p t -> p c t"), ["QT1"], [qk_])
            DMA("sp", k_, KT1[:, :, t0:t0 + TT].rearrange("c p t -> p c t"), ["KT1"], [kk_])
            for ch in range(4):
                DMA("sp", v_[:, ch, :, 0:128],
                    VT1[t0 + ch * 128:t0 + (ch + 1) * 128, :].rearrange("i (h e) -> i h e", e=128), ["VT1"], [vk_])
            DMA("sp", g_, GS1[t0:t0 + TT, :].rearrange("(s i) c -> i s c", i=128), ["GS1"], [gk_])
            if bwd:
                DMA("sp", hfg[bi], HF[:, :, t0:t0 + TT].rearrange("c p t -> p c t"), ["HF"], ["hfg"])
                DMA("sp", osg[bi], OS1[:, :, t0:t0 + TT].rearrange("c p t -> p c t"), ["OS1"], ["osg"])
            for ci in range(4):
                ch = 3 - ci if bwd else ci
                chunk = g * 4 + ch
                cs_ = slice(ch * 128, (ch + 1) * 128)
                lf = g_[:, ch, lfo:lfo + 8]
                ig = g_[:, ch, igo:igo + 8]
                sm = ps[7][:, 256:272]
                MM(sm[:, 0:8], Tri, lf, True, True, ["cf", gk_], [("B", 7)])
                MM(sm[:, 8:16], onesf, lf, True, True, ["cf", gk_], [("B", 7)])
                TTOP("dve", asb, ig, sm[:, 0:8], ALU.subtract, [gk_, ("B", 7)], ["asb"])
                TTOP("dve", wsb, asb, sm[:, 8:16], ALU.add, ["asb", ("B", 7)], ["wsb"])
                ACT(wsb, wsb, AF.Exp, ["wsb"], ["wsb"])
                ACT(dcy, sm[:, 8:16], AF.Exp, [("B", 7)], ["dcy"])
                for hh in range(2):
                    heads = range(hh * 4, hh * 4 + 4)
                    for h in heads:
                        u = h % 4
                        pa = ps[u // 2]
                        co = (u % 2) * 256
                        lfb = lf[:, h:h + 1].to_broadcast([128, 128])
                        MM(pa[:, co:co + 128], lfb, Tri, True, True, [gk_, "cf"], [("B", u // 2)])
                        MM(pa[:, co + 128:co + 256], lfb, Tri, True, False, [gk_, "cf"], [("B", u // 2)])
                        MM(pa[:, co + 128:co + 256], identf, mskb, False, True, ["cf"], [("B", u // 2)])
                        MM(ps[2][:, u * 128:(u + 1) * 128], k_[:, h, cs_], q_[:, h, cs_], True, True,
                           [kk_, qk_], [("B", 2)])
                        TR(psT[:, u * 128:(u + 1) * 128], k_[:, h, cs_], ident_b, [kk_, "cb"], [("B", 7)])
                    for h in heads:
                        u = h % 4
                        pa = ps[u // 2]
                        co = (u % 2) * 256
                        ACT(eb[h], pa[:, co:co + 128], AF.Exp, [("B", u // 2)], [f"eb{h}"])
                        ACT(dtl[h], pa[:, co + 128:co + 256], AF.Exp, [("B", u // 2), "asb"], [f"dt{h}"],
                            scale=1.0, bias=asb[:, h:h + 1])
                        ACT(kwb[h], psT[:, u * 128:(u + 1) * 128], AF.Identity, [("B", 7), "wsb"], [f"kwb{h}"],
                            scale=wsb[:, h:h + 1])
                    for h in heads:
                        u = h % 4
                        TTOP("dve", scb[h], ps[2][:, u * 128:(u + 1) * 128], dtl[h], ALU.mult,
                             [("B", 2), f"dt{h}"], [f"scb{h}"])
                        TTOP("pool", qtb[h], q_[:, h, cs_], eb[h], ALU.mult, [qk_, f"eb{h}"], [f"qtb{h}"])
                    for h in heads:
                        u = h % 4
                        pn = ps[3 + u // 2]
                        co = (u % 2) * 256
                        MM(pn[:, co:co + 128], v_[:, ch, h, 0:128], scb[h], True, False, [vk_, f"scb{h}"], [("B", 3 + u // 2)])
                        MM(pn[:, co:co + 128], CNb[:, h, 0:128], qtb[h], False, True, [("CNb", h), f"qtb{h}"], [("B", 3 + u // 2)])
                        MM(pn[:, co + 128:co + 256], ones_b, scb[h], True, False, ["cb", f"scb{h}"], [("B", 3 + u // 2)])
                        MM(pn[:, co + 128:co + 256], CNb[:, h, 128:256], qtb[h], False, True,
                           [("CNb", h), f"qtb{h}"], [("B", 3 + u // 2)])
                        pu = ps[5 + u // 2]
                        MM(pu[:, co:co + 256], kwb[h], v_[:, ch, h, :], True, True, [f"kwb{h}", vk_], [("B", 5 + u // 2)])
                    for h in heads:
                        u = h % 4
                        pn = ps[3 + u // 2]
                        co = (u % 2) * 256
                        ACT(rr[h], pn[:, co + 128:co + 256], AF.Abs, [("B", 3 + u // 2)], [f"rr{h}"])
                        TS("dve", rr[h], rr[h], 1.0, None, ALU.max, None, [f"rr{h}"], [f"rr{h}"])
                        RECIP(rr[h], rr[h], [f"rr{h}"], [f"rr{h}"])
                        TTOP("dve", hs_[:, h, cs_], pn[:, co:co + 128], rr[h], ALU.mult, [("B", 3 + u // 2), f"rr{h}"], [(hk_, h)])
                        pu = ps[5 + u // 2]
                        STT("dve", CN[:, h, :], CN[:, h, :], dcy[:, h:h + 1], pu[:, co:co + 256], ALU.mult, ALU.add,
                            [("B", 5 + u // 2), "dcy", ("CN", h)], [("CN", h)])
                segb = (chunk % 16 == 0) if bwd else (chunk % 16 == 15)
                seg = chunk // 16
                nseg = seg - 1 if bwd else seg + 1
                allCN = [("CN", h) for h in range(8)]
                if segb and 0 <= nseg < NSEG:
                    fcol = fl[:, 1, nseg:nseg + 1] if bwd else fl[:, 0, nseg:nseg + 1]
                    TS("dve", CN, CN, fcol, None, ALU.mult, None, allCN + ["fl"], allCN)
                for h in range(8):
                    CP("act" if h % 2 == 0 else "pool", CNb[:, h, :], CN[:, h, :], [("CN", h)], [("CNb", h)])
            if not bwd:
                DMA("sp", HF[:, :, t0:t0 + TT].rearrange("c p t -> p c t"), hs_, [(hk_, h) for h in range(8)], ["HF"])
            else:
                hf_, os_, go_ = hfg[bi], osg[bi], gto[bi]
                for h in range(8):
                    b = h % 2
                    bkeys = [("B", b)]
                    TTOP("dve", hs_[:, h, :], hs_[:, h, :], hf_[:, h, :], ALU.add, [(hk_, h), "hfg"], [(hk_, h)])
                    ACT(sqb, hs_[:, h, :], AF.Square, [(hk_, h)], ["sqb"])
                    MM(ps[b], ones_b, sqb, True, True, ["sqb", "cb"], bkeys)
                    ACT(rs, ps[b], AF.Sqrt, bkeys, ["rs"], scale=1.0 / 128, bias=cx.eps_col)
                    RECIP(rs, rs, ["rs"], ["rs"])
                    TTOP("dve", hs_[:, h, :], hs_[:, h, :], rs, ALU.mult, [(hk_, h), "rs"], [(hk_, h)])
                    STT("dve", go_[:, h, :], hs_[:, h, :], hn[:, h:h + 1], os_[:, h, :], ALU.mult, ALU.mult,
                        [(hk_, h), "hn", "osg"], ["gto"])
                DMA("sp", GT[:, :, t0:t0 + TT].rearrange("c p t -> p c t"), go_, ["gto"], ["GT"])

    phase_scan(False)
    phase_scan(True)
    phase_outproj(c_w_out, [(GT, 8, 0, "GT")], GG1[1])
    phase_mlp(1, True)


def make_consts():
    c = np.zeros((128, 128 * 8 + 512), np.float32)
    i = np.arange(128)
    o = 0
    c[:, o:o + 128] = np.eye(128); o += 128
    pm = np.zeros((128, 128), np.float32)
    for m in range(128):
        r = m % 64
        if r < 8:
            pm[m + 8, m] = 1.0
        elif r < 16:
            pm[m - 8, m] = 1.0
    c[:, o:o + 128] = pm; o += 128
    c[:, o:o + 128] = (i[:, None] <= i[None, :]); o += 128
    c[:, o:o + 128] = (i[:, None] >= i[None, :]); o += 128
    c[:, o:o + 128] = np.where(i[:, None] <= i[None, :], 0.0, NEG); o += 128
    c[:, o:o + 128] = np.where(i[:, None] >= i[None, :], 0.0, NEG); o += 128
    c[:, o:o + 128] = 1.0; o += 128
    c[:, o:o + 128] = 0.0; o += 128
    ma = np.where(i[:, None] >= i[None, :], 0.0, NEG)
    mb = np.where(i[:, None] <= i[None, :], 0.0, NEG)
    c[:, o:o + 256] = np.concatenate([ma, ma], 1); o += 256
    c[:, o:o + 256] = np.concatenate([mb, mb], 1); o += 256
    return c


def rope_tables(pos):
    half = 8
    inv = np.power(np.float32(500000.0), -np.arange(half, dtype=np.float32) / half).astype(np.float32)
    ang = pos.astype(np.float32)[None, :] * inv[:, None]
    cos = np.cos(ang).astype(np.float32)
    sin = np.sin(ang).astype(np.float32)
    Tn = pos.shape[0]
    ct = np.ones((128, Tn), np.float32)
    st = np.zeros((128, Tn), np.float32)
    for hb in (0, 64):
        ct[hb:hb + 8] = cos
        ct[hb + 8:hb + 16] = cos
        st[hb:hb + 8] = -sin
        st[hb + 8:hb + 16] = sin
    return ct, st


def core_inputs(xseg, cseg, pos, lo, hi, w):
    nseg = cseg.shape[0]
    f32 = np.float32
    m = {}
    m["x"] = np.ascontiguousarray(xseg, f32)
    m["cT"] = np.ascontiguousarray(cseg.T.reshape(8, 128, nseg).transpose(1, 0, 2), f32)
    m["ropec"], m["ropes"] = rope_tables(pos)
    fl = np.zeros((128, 4, nseg), f32)
    fl[:, 0, :] = lo[None, :]
    fl[:, 1, :] = hi[None, :]
    k = np.arange(128)
    fl[:, 2, :] = np.where((k[:, None] < 64) & (lo[None, :] == 0), NEG, 0.0)
    fl[:, 3, :] = np.where((k[:, None] >= 64) & (hi[None, :] == 0), NEG, 0.0)
    m["flags"] = fl
    m.update(w)
    return m


def weight_inputs(inp):
    f32 = np.float32
    w = {}
    w["ada_w"] = np.ascontiguousarray(inp["ada_w"], f32)
    w["ada_bT"] = np.ascontiguousarray(np.asarray(inp["ada_b"], f32).reshape(2, 48, 128).transpose(2, 0, 1))
    nr = np.stack([inp["norm_mix"][0], inp["norm_mix"][1], inp["norm_mlp"][0], inp["norm_mlp"][1],
                   inp["norm_final"]], 0).astype(f32)
    w["norms"] = np.ascontiguousarray(nr.reshape(5, 8, 128).transpose(2, 0, 1))
    w["ab_w_in"] = np.ascontiguousarray(inp["ab_w_in"][0], f32)
    w["conv_wT"] = np.ascontiguousarray(np.asarray(inp["conv_w"][0], f32).T.reshape(4, 128, 31).transpose(1, 0, 2))
    cv = np.stack([inp["conv_b"][0], inp["conv_ln_g"][0], inp["conv_ln_b"][0]], 0).astype(f32)
    w["conv_vec"] = np.ascontiguousarray(cv.reshape(3, 4, 128).transpose(2, 0, 1))
    w["ab_w_out"] = np.ascontiguousarray(inp["ab_w_out"][0], f32)
    w["c_w_in"] = np.ascontiguousarray(inp["c_w_in"][0], f32)
    w["gate_b"] = np.ascontiguousarray(np.broadcast_to(np.asarray(inp["c_gate_b"][0], f32)[None, :], (128, 32)))
    w["head_norm"] = np.ascontiguousarray(np.asarray(inp["c_head_norm"][0], f32).reshape(8, 128).T)
    w["c_w_out"] = np.ascontiguousarray(inp["c_w_out"][0], f32)
    w["mlp_w1"] = np.ascontiguousarray(inp["mlp_w1"], f32)
    w["mlp_w2"] = np.ascontiguousarray(inp["mlp_w2"], f32)
    w["consts"] = make_consts()
    return w


_CACHE = {}


def kernel(**inputs):
    inp = {k: np.asarray(v) for k, v in inputs.items()}
    NSEG = 8
    xp, xs_ = inp["x_prompt"], inp["x_sample"]
    cp, cs_ = inp["c_prompt"], inp["c_sample"]
    w = weight_inputs(inp)
    in_maps = []
    T = NSEG * SEG
    for b in range(2):
        lo = np.ones(NSEG, np.float32); lo[0] = 0
        hi = np.ones(NSEG, np.float32); hi[-1] = 0
        in_maps.append(core_inputs(xp[b], np.broadcast_to(cp[b][None, :], (NSEG, D)), np.arange(T), lo, hi, w))
    for c in range(6):
        cc = c % 4
        sl = slice(cc * 8, cc * 8 + 8)
        z = np.zeros(NSEG, np.float32)
        in_maps.append(core_inputs(xs_[sl].reshape(T, D), cs_[sl], np.tile(np.arange(SEG), NSEG), z, z, w))
    if "nc" not in _CACHE:
        _CACHE["nc"] = build(NSEG)[0]
    nc = _CACHE["nc"]
    res = run_bass_kernel_spmd(nc, in_maps, core_ids=list(range(8)))
    yp = np.stack([res.results[b]["y"].reshape(16384, D) for b in range(2)], 0).astype(np.float32)
    ys = np.concatenate([res.results[2 + c]["y"].reshape(8, SEG, D) for c in range(4)], 0).astype(np.float32)
    return (yp, ys)
```

```python
import numpy as np
import concourse.bass as bass
import concourse.mybir as mybir
from concourse.bass_utils import run_bass_kernel_spmd

F32 = mybir.dt.float32
BF16 = mybir.dt.bfloat16
AF = mybir.ActivationFunctionType
ALU = mybir.AluOpType

SEG = 2048
D = 1024
TT = 512
PADK = 1024
PADC = 16
EPS = 1e-6
NEG = -30000.0
PATTERNS = ((128, 1), (512, 4), (2048, 16))

QUEUES = ("sp", "pe", "act", "dve", "pool")
SEM_CAP = 30000
DMA_RING = 12


class Op:
    __slots__ = ("q", "fn", "deps", "signal", "idx", "sig_epoch", "sig_val",
                 "dma", "dma_i")

    def __init__(self, q, fn, dma):
        self.q = q
        self.fn = fn
        self.dma = dma
        self.deps = set()
        self.signal = False
        self.sig_epoch = 0
        self.sig_val = 0
        self.dma_i = -1


class Sched:
    def __init__(self, nc):
        self.nc = nc
        self.ops = []
        self.last_w = {}
        self.readers = {}
        self.pending = {q: None for q in QUEUES}
        self.last_c = {q: None for q in QUEUES}
        self.last_d = {q: [] for q in QUEUES}

    def barrier(self):
        b = set()
        for q in QUEUES:
            if self.last_c[q] is not None:
                b.add(self.last_c[q])
            b.update(self.last_d[q][-DMA_RING:])
        for q in QUEUES:
            self.pending[q] = set(b) | (self.pending[q] or set())
        self.last_w = {}
        self.readers = {}

    def add(self, q, fn, reads=(), writes=(), dma=False):
        op = Op(q, fn, dma)
        op.idx = len(self.ops)
        deps = set()
        for k in reads:
            w = self.last_w.get(k)
            if w is not None:
                deps.add(w)
        for k in writes:
            w = self.last_w.get(k)
            if w is not None:
                deps.add(w)
            for r in self.readers.get(k, ()):
                deps.add(r)
        if self.pending[q] is not None:
            deps |= self.pending[q]
            self.pending[q] = None
        for d in deps:
            dop = self.ops[d]
            if dop.q == "pe" and q == "pe" and not dop.dma and not dma:
                continue
            op.deps.add(d)
        for k in writes:
            self.last_w[k] = op.idx
            self.readers[k] = []
        for k in reads:
            if k in writes:
                continue
            self.readers.setdefault(k, []).append(op.idx)
        if dma:
            self.last_d[q].append(op.idx)
        else:
            self.last_c[q] = op.idx
        self.ops.append(op)
        return op

    def emit(self):
        nc = self.nc
        ops = self.ops
        for op in ops:
            for d in op.deps:
                ops[d].signal = True
        cnt = {q: 0 for q in QUEUES}
        dcnt = {q: 0 for q in QUEUES}
        n_epochs = {q: 0 for q in QUEUES}
        for op in ops:
            if op.dma:
                op.dma_i = dcnt[op.q]
                dcnt[op.q] += 1
            elif op.signal:
                c = cnt[op.q]
                op.sig_epoch = c // SEM_CAP
                op.sig_val = c % SEM_CAP + 1
                cnt[op.q] = c + 1
                n_epochs[op.q] = op.sig_epoch + 1
        csems = {q: [nc.alloc_semaphore(f"c_{q}_{i}") for i in range(n_epochs[q])]
                 for q in QUEUES}
        dsems = {q: [nc.alloc_semaphore(f"d_{q}_{i}")
                     for i in range(min(DMA_RING, dcnt[q]))] for q in QUEUES}
        per_q = {q: [] for q in QUEUES}
        for op in ops:
            per_q[op.q].append(op)

        def target(dop):
            if not dop.dma:
                return (csems[dop.q][dop.sig_epoch], dop.sig_val,
                        ("c", dop.q, dop.sig_epoch))
            i = dop.dma_i
            return (dsems[dop.q][i % DMA_RING], 16 * (i // DMA_RING + 1),
                    ("d", dop.q, i % DMA_RING))

        def run_queue(qname, eng):
            waited = {}
            for op in per_q[qname]:
                need = {}
                for d in op.deps:
                    s, v, key = target(ops[d])
                    if waited.get(key, 0) >= v:
                        continue
                    if key not in need or need[key][1] < v:
                        need[key] = (s, v)
                if op.dma and op.dma_i >= DMA_RING:
                    i = op.dma_i - DMA_RING
                    key = ("d", qname, i % DMA_RING)
                    v = 16 * (i // DMA_RING + 1)
                    if waited.get(key, 0) < v and (key not in need or need[key][1] < v):
                        need[key] = (dsems[qname][i % DMA_RING], v)
                for key, (s, v) in need.items():
                    eng.wait_ge(s, v)
                    waited[key] = v
                ins = op.fn(eng)
                if op.dma:
                    ins.then_inc(dsems[qname][op.dma_i % DMA_RING], 16)
                elif op.signal:
                    ins.then_inc(csems[qname][op.sig_epoch], 1)
            n = dcnt[qname]
            for r in range(min(DMA_RING, n)):
                last_i = ((n - 1 - r) // DMA_RING) * DMA_RING + r
                v = 16 * (last_i // DMA_RING + 1)
                key = ("d", qname, r)
                if waited.get(key, 0) < v:
                    eng.wait_ge(dsems[qname][r], v)

        with nc.Block() as block:
            @block.sync
            def _(e):
                run_queue("sp", e)

            @block.tensor
            def _(e):
                run_queue("pe", e)

            @block.scalar
            def _(e):
                run_queue("act", e)

            @block.vector
            def _(e):
                run_queue("dve", e)

            @block.gpsimd
            def _(e):
                run_queue("pool", e)
        self.stats = {q: len(per_q[q]) for q in QUEUES}


class Arena:
    def __init__(self, nc, words):
        self.t = nc.alloc_sbuf_tensor("arena", [128, words], F32)
        self.words = words
        self.base = 0
        self.off = 0
        self.gen = 0

    def freeze(self):
        self.base = self.off

    def reset(self):
        self.off = self.base
        self.gen += 1

    def alloc(self, name, shape, dtype):
        n = 1
        for s in shape[1:]:
            n *= s
        w = n if dtype == F32 else (n + 1) // 2
        w = (w + 7) // 8 * 8
        assert self.off + w <= self.words, f"arena overflow at {name}: {self.off + w} > {self.words}"
        v = self.t[:, self.off:self.off + w]
        self.off += w
        if dtype != F32:
            v = v.bitcast(dtype)
        v = v[:, 0:n]
        if len(shape) == 3:
            v = v.rearrange("p (a b) -> p a b", a=shape[1])
        elif len(shape) == 4:
            v = v.rearrange("p (a b c) -> p a b c", a=shape[1], b=shape[2])
        return v


class Cx:
    pass


def _emitters(cx):
    S = cx.S

    def MM(out, lhsT, rhs, start, stop, R, W):
        S.add("pe", lambda e: e.matmul(out, lhsT=lhsT, rhs=rhs, start=start, stop=stop), R, W)

    def TR(out, in_, ident, R, W):
        S.add("pe", lambda e: e.transpose(out, in_, ident), R, W)

    def ACT(out, in_, func, R, W, scale=1.0, bias=0.0):
        S.add("act", lambda e: e.activation(out=out, in_=in_, func=func, bias=bias, scale=scale), R, W)

    def TTOP(q, out, in0, in1, op, R, W):
        S.add(q, lambda e: e.tensor_tensor(out=out, in0=in0, in1=in1, op=op), R, W)

    def STT(q, out, in0, scalar, in1, op0, op1, R, W):
        S.add(q, lambda e: e.scalar_tensor_tensor(out=out, in0=in0, scalar=scalar, in1=in1,
                                                  op0=op0, op1=op1), R, W)

    def TS(q, out, in0, s1, s2, op0, op1, R, W):
        if s2 is None:
            S.add(q, lambda e: e.tensor_scalar(out=out, in0=in0, scalar1=s1, scalar2=None, op0=op0), R, W)
        else:
            S.add(q, lambda e: e.tensor_scalar(out=out, in0=in0, scalar1=s1, scalar2=s2,
                                               op0=op0, op1=op1), R, W)

    def CP(q, out, in_, R, W):
        if q == "act":
            S.add("act", lambda e: e.copy(out=out, in_=in_), R, W)
        else:
            S.add(q, lambda e: e.tensor_copy(out=out, in_=in_), R, W)

    def RECIP(out, in_, R, W):
        S.add("dve", lambda e: e.reciprocal(out=out, in_=in_), R, W)

    def MEMSET(q, out, val, W):
        S.add(q, lambda e: e.memset(out, val), (), W)

    def DMA(q, out, in_, R, W):
        S.add(q, lambda e: e.dma_start(out=out, in_=in_), R, W, dma=True)

    cx.MM, cx.TR, cx.ACT, cx.TTOP, cx.STT, cx.TS, cx.CP = MM, TR, ACT, TTOP, STT, TS, CP
    cx.RECIP, cx.MEMSET, cx.DMA = RECIP, MEMSET, DMA


def dsl(start, d, n=128):
    return slice(start, start + d * (n - 1) + 1, d)


class Rot:
    def __init__(self, items):
        self.items = list(items)
        self.i = 0

    def __call__(self):
        v = self.items[self.i % len(self.items)]
        self.i += 1
        return v


def load_weight(cx, dst, src, kcs, ncols, key, coff=0, stage_words=2048, st=None):
    A = cx.A
    own = st is None
    if st is None:
        st = [A.alloc(f"wst{i}_{A.off}", [128, stage_words], F32) for i in range(2)]
    engs = Rot(["dve", "pool", "act"])
    i = 0
    stkey = key if own else "stg"
    for kc in range(kcs):
        for c0 in range(0, ncols, stage_words):
            n = min(stage_words, ncols - c0)
            s = st[i % 2]
            sk = f"{stkey}_st{i % 2}"
            cx.DMA("sp", s[:, 0:n], src[kc * 128:(kc + 1) * 128, coff + c0:coff + c0 + n], [], [sk])
            cx.CP(engs(), dst[:, kc, c0:c0 + n], s[:, 0:n], [sk], [key])
            i += 1


def rms_to_hT(cx, xT, hT, G, SH, seg, sq, tmp, rstd, psb, n=512, nch=8, inv=1.0 / 1024):
    pb = cx.ps[psb]
    for c in range(nch):
        s = sq[c % 2]
        cx.ACT(s[:, 0:n], xT[:, c, 0:n], AF.Square, [("xT", c)], [f"sq{c % 2}"])
        cx.MM(pb[:, 0:n], cx.ones_b, s[:, 0:n], c == 0, c == nch - 1, [f"sq{c % 2}"], [f"ps{psb}"])
    cx.ACT(rstd[:, 0:n], pb[:, 0:n], AF.Sqrt, [f"ps{psb}"], ["rstd"], scale=inv, bias=cx.eps_col)
    cx.RECIP(rstd[:, 0:n], rstd[:, 0:n], ["rstd"], ["rstd"])
    for c in range(nch):
        t = tmp[c % 2]
        cx.STT("dve", t[:, 0:n], xT[:, c, 0:n], G[:, c, seg:seg + 1], rstd[:, 0:n], ALU.mult, ALU.mult,
               [("xT", c), "rstd"], [f"tmpn{c % 2}"])
        cx.ACT(hT[:, c, 0:n], t[:, 0:n], AF.Identity, [f"tmpn{c % 2}"], [("hT", c)],
               scale=1.0, bias=SH[:, c, seg:seg + 1])


class _Stop(Exception):
    pass


def build(NSEG, debug=False, upto=99, sub=0):
    nc, cx = _build_head(NSEG, debug)
    cx.upto = upto
    cx.sub = sub
    cx.phase_no = 0
    try:
        _build_body(nc, cx, NSEG)
    except _Stop:
        pass
    cx.S.emit()
    cx.stats = cx.S.stats
    return nc, cx


def _build_head(NSEG, debug):
    nc = bass.Bass("TRN2", target_bir_lowering=False)
    cx = Cx()
    cx.nc = nc
    cx.S = Sched(nc)
    cx.debug = debug
    _emitters(cx)
    return nc, cx


def _build_body(nc, cx, NSEG):
    debug = cx.debug
    T = NSEG * SEG
    NT = T // TT
    S = cx.S
    _orig_barrier = S.barrier

    def gated_barrier():
        cx.phase_no += 1
        if cx.phase_no > cx.upto:
            raise _Stop()
        _orig_barrier()
    S.barrier = gated_barrier

    def ckpt(n):
        if cx.sub == n:
            raise _Stop()

    def din(name, shape):
        return nc.dram_tensor(name, list(shape), F32, kind="ExternalInput").ap()

    def dscr(name, shape, dt):
        skind = "ExternalOutput" if (debug and name in debug) else "Internal"
        return nc.dram_tensor(name, list(shape), dt, kind=skind).ap()

    x = din("x", [T, D])
    cT = din("cT", [128, 8, NSEG])
    ada_w = din("ada_w", [2, D, 6 * D])
    ada_bT = din("ada_bT", [128, 2, 48])
    norms = din("norms", [128, 5, 8])
    ab_w_in = din("ab_w_in", [D, 5632])
    conv_wT = din("conv_wT", [128, 4, 31])
    conv_vec = din("conv_vec", [128, 3, 4])
    ab_w_out = din("ab_w_out", [D, D])
    c_w_in = din("c_w_in", [D, 4128])
    gate_b = din("gate_b", [128, 32])
    head_norm = din("head_norm", [128, 8])
    c_w_out = din("c_w_out", [D, D])
    mlp_w1 = din("mlp_w1", [2, D, 4 * D])
    mlp_w2 = din("mlp_w2", [2, 4 * D, D])
    ropec = din("ropec", [128, T])
    ropes = din("ropes", [128, T])
    flags = din("flags", [128, 4, NSEG])
    NCONST = 128 * 8 + 256 * 2
    consts = din("consts", [128, NCONST])
    y = nc.dram_tensor("y", [T, D], F32, kind="ExternalOutput").ap()

    XT = dscr("XT", [8, 128, T], F32)
    AGLU = dscr("AGLU", [4, 128, T + 2 * PADC], F32)
    QKs = dscr("QKs", [3, 2, 4, 128, T + 2 * PADK], BF16)
    Vs = dscr("Vs", [3, T + 2 * PADK, 512], BF16)
    ACV = dscr("ACV", [4, 128, T], BF16)
    ATT = dscr("ATT", [4, 128, T], BF16)
    QT1 = dscr("QT1", [8, 128, T], BF16)
    KT1 = dscr("KT1", [8, 128, T], BF16)
    OS1 = dscr("OS1", [8, 128, T], BF16)
    VT1 = dscr("VT1", [T, D], BF16)
    GS1 = dscr("GS1", [T, 32], F32)
    HF = dscr("HF", [8, 128, T], F32)
    GT = dscr("GT", [8, 128, T], BF16)
    cx.dbg = dict(XT=XT, AGLU=AGLU, QKs=QKs, Vs=Vs, ACV=ACV, ATT=ATT, QT1=QT1, KT1=KT1, OS1=OS1,
                  VT1=VT1, GS1=GS1, HF=HF, GT=GT)

    A = Arena(nc, 53000)
    cx.A = A
    cx.ps = [nc.alloc_psum_tensor(f"ps{i}", [128, 512], F32)[:] for i in range(8)]
    ps = cx.ps
    MM, TR, ACT, TTOP, STT, TS, CP, RECIP, MEMSET, DMA = (cx.MM, cx.TR, cx.ACT, cx.TTOP, cx.STT,
                                                          cx.TS, cx.CP, cx.RECIP, cx.MEMSET, cx.DMA)

    cf = A.alloc("cf", [128, NCONST], F32)
    DMA("sp", cf, consts, [], ["cf"])
    o = 0
    identf = cf[:, o:o + 128]; o += 128
    Pm_f = cf[:, o:o + 128]; o += 128
    Utri = cf[:, o:o + 128]; o += 128
    Ltri = cf[:, o:o + 128]; o += 128
    maskU_f = cf[:, o:o + 128]; o += 128
    maskL_f = cf[:, o:o + 128]; o += 128
    onesf = cf[:, o:o + 128]; o += 128
    zerosf = cf[:, o:o + 128]; o += 128
    maskA_f = cf[:, o:o + 256]; o += 256
    maskB_f = cf[:, o:o + 256]; o += 256
    cb = A.alloc("cb", [128, 128 * 5 + 512], BF16)
    CP("dve", cb[:, 0:128], identf, ["cf"], ["cb"])
    CP("dve", cb[:, 128:256], Pm_f, ["cf"], ["cb"])
    CP("dve", cb[:, 256:384], maskU_f, ["cf"], ["cb"])
    CP("dve", cb[:, 384:512], maskL_f, ["cf"], ["cb"])
    CP("dve", cb[:, 512:640], onesf, ["cf"], ["cb"])
    CP("dve", cb[:, 640:896], maskA_f, ["cf"], ["cb"])
    CP("dve", cb[:, 896:1152], maskB_f, ["cf"], ["cb"])
    ident_b = cb[:, 0:128]
    Pm_b = cb[:, 128:256]
    maskU_b = cb[:, 256:384]
    maskL_b = cb[:, 384:512]
    cx.ones_b = ones_b = cb[:, 512:640]
    maskA_b = cb[:, 640:896]
    maskB_b = cb[:, 896:1152]
    epsc = A.alloc("epsc", [128, 8], F32)
    MEMSET("dve", epsc, EPS, ["epsc"])
    cx.eps_col = epsc[:, 0:1]
    fl = A.alloc("fl", [128, 4, NSEG], F32)
    DMA("sp", fl, flags, [], ["fl"])
    nrm = A.alloc("nrm", [128, 5, 8], F32)
    DMA("sp", nrm, norms, [], ["nrm"])
    cwt = A.alloc("cwt", [128, 4, 31], F32)
    DMA("sp", cwt, conv_wT, [], ["cwt"])
    cvec = A.alloc("cvec", [128, 3, 4], F32)
    DMA("sp", cvec, conv_vec, [], ["cvec"])
    gb = A.alloc("gb", [128, 32], F32)
    DMA("sp", gb, gate_b, [], ["gb"])
    hn = A.alloc("hn", [128, 8], F32)
    DMA("sp", hn, head_norm, [], ["hn"])
    abT = A.alloc("abT", [128, 2, 48], F32)
    DMA("sp", abT, ada_bT, [], ["abT"])
    modT = A.alloc("modT", [128, 2, 48, NSEG], F32)
    G1 = [A.alloc(f"G1_{l}", [128, 8, NSEG], F32) for l in range(2)]
    G2 = [A.alloc(f"G2_{l}", [128, 8, NSEG], F32) for l in range(2)]
    A.freeze()

    A.reset()
    cs = A.alloc("cs", [128, 8, NSEG], F32)
    DMA("sp", cs, cT, [], ["cs"])
    ACT(cs, cs, AF.Silu, ["cs"], ["cs"])
    wada = [A.alloc(f"wada{i}", [128, 8, 1024], F32) for i in range(2)]
    it = 0
    for l in range(2):
        for fb in range(6):
            wt = wada[it % 2]
            wk = f"wada{it % 2}"
            it += 1
            for kc in range(8):
                DMA("sp", wt[:, kc, :], ada_w[l, kc * 128:(kc + 1) * 128, fb * 1024:(fb + 1) * 1024], [], [wk])
            pb = fb % 2
            for f in range(8):
                for kc in range(8):
                    MM(ps[pb][:, f * NSEG:(f + 1) * NSEG], wt[:, kc, f * 128:(f + 1) * 128], cs[:, kc, :],
                       kc == 0, kc == 7, [wk, "cs"], [f"ps{pb}"])
            for f in range(8):
                fi = fb * 8 + f
                ACT(modT[:, l, fi, :], ps[pb][:, f * NSEG:(f + 1) * NSEG], AF.Identity, [f"ps{pb}", "abT"],
                    ["modT"], scale=1.0, bias=abT[:, l, fi:fi + 1])
    for l in range(2):
        for c in range(8):
            TS("dve", G1[l][:, c, :], modT[:, l, 8 + c, :], 1.0, nrm[:, l, c:c + 1], ALU.add, ALU.mult,
               ["modT", "nrm"], ["G"])
            TS("dve", G2[l][:, c, :], modT[:, l, 32 + c, :], 1.0, nrm[:, 2 + l, c:c + 1], ALU.add, ALU.mult,
               ["modT", "nrm"], ["G"])
    SH1 = [modT[:, l, 0:8, :] for l in range(2)]
    GG1 = [modT[:, l, 16:24, :] for l in range(2)]
    SH2 = [modT[:, l, 24:32, :] for l in range(2)]
    GG2 = [modT[:, l, 40:48, :] for l in range(2)]

    zt = A.alloc("zt", [128, 4, PADC], F32)
    MEMSET("pool", zt, 0.0, ["zt"])
    zb = A.alloc("zb", [128, 4, PADK], BF16)
    MEMSET("pool", zb, 0.0, ["zb"])
    zb8 = zb.rearrange("p a b -> p (a b)").rearrange("p (a c) -> p a c", c=512)
    DMA("sp", AGLU[:, :, 0:PADC].rearrange("c p t -> p c t"), zt, ["zt"], ["AGLU"])
    DMA("sp", AGLU[:, :, PADC + T:PADC + T + PADC].rearrange("c p t -> p c t"), zt, ["zt"], ["AGLU"])
    for p in range(3):
        for part in range(2):
            DMA("sp", QKs[p, part, :, :, 0:PADK].rearrange("c p t -> p c t"), zb, ["zb"], ["QKs"])
            DMA("sp", QKs[p, part, :, :, PADK + T:PADK + T + PADK].rearrange("c p t -> p c t"), zb, ["zb"], ["QKs"])
        for r0 in (0, PADK + T):
            DMA("sp", Vs[p, r0:r0 + PADK, :].rearrange("(a p) c -> p a c", p=128), zb8, ["zb"], ["Vs"])

    S.barrier()
    A.reset()
    W = A.alloc("w_in", [128, 8, 5632], BF16)
    load_weight(cx, W, ab_w_in, 8, 5632, "w_in")
    xs = A.alloc("xs", [128, 4, D], F32)
    xT = A.alloc("xT", [128, 8, TT], F32)
    sq = [A.alloc(f"sq{i}", [128, TT], BF16) for i in range(2)]
    rstd = A.alloc("rstd", [128, TT], F32)
    tmpn = [A.alloc(f"tmpn{i}", [128, TT], F32) for i in range(2)]
    hT = A.alloc("hT", [128, 8, TT], BF16)
    cosT = A.alloc("cosT", [128, TT], F32)
    sinT = A.alloc("sinT", [128, TT], F32)
    sig = [A.alloc(f"sig{i}", [128, TT], F32) for i in range(2)]
    ag = [A.alloc(f"ag{i}", [128, TT], F32) for i in range(2)]
    qb = [A.alloc(f"qb{i}", [128, TT], BF16) for i in range(2)]
    t1 = [A.alloc(f"t1{i}", [128, TT], F32) for i in range(2)]
    t2 = [A.alloc(f"t2{i}", [128, TT], F32) for i in range(2)]
    qo = [A.alloc(f"qo{i}", [128, TT], BF16) for i in range(2)]
    vst = A.alloc("vst", [128, 4, 1536], BF16)
    bank = Rot([0, 1, 2, 3])
    bank2 = Rot([4, 5])
    ev = Rot(["act", "dve"])
    DMA("sp", xs, x[0:TT, :].rearrange("(s p) d -> p s d", p=128), [], ["xs"])
    for tt in range(NT):
        ckpt(10 + tt)
        seg = tt // 4
        t0 = tt * TT
        DMA("sp", cosT, ropec[:, t0:t0 + TT], [], ["cosT"])
        DMA("sp", sinT, ropes[:, t0:t0 + TT], [], ["sinT"])
        for c in range(8):
            b = bank()
            for s in range(4):
                TR(ps[b][:, s * 128:(s + 1) * 128], xs[:, s, c * 128:(c + 1) * 128], identf, ["xs", "cf"], [f"ps{b}"])
            CP(ev(), xT[:, c, :], ps[b], [f"ps{b}"], [("xT", c)])
        if tt + 1 < NT:
            DMA("sp", xs, x[t0 + TT:t0 + 2 * TT, :].rearrange("(s p) d -> p s d", p=128), [], ["xs"])
        DMA("sp", XT[:, :, t0:t0 + TT].rearrange("c p t -> p c t"), xT, [("xT", c) for c in range(8)], ["XT"])
        ckpt(1)
        rms_to_hT(cx, xT, hT, G1[0], SH1[0], seg, sq, tmpn, rstd, 6)
        ckpt(2)
        hk = [("hT", c) for c in range(8)]
        for c in range(4):
            bl = bank()
            for kc in range(8):
                MM(ps[bl], W[:, kc, c * 128:(c + 1) * 128], hT[:, kc, :], kc == 0, kc == 7, ["w_in"] + hk, [f"ps{bl}"])
            bg = bank()
            for kc in range(8):
                MM(ps[bg], W[:, kc, 512 + c * 128:512 + (c + 1) * 128], hT[:, kc, :], kc == 0, kc == 7,
                   ["w_in"] + hk, [f"ps{bg}"])
            i = c % 2
            ACT(sig[i], ps[bg], AF.Sigmoid, [f"ps{bg}"], [f"sig{i}"])
            TTOP("dve", ag[i], ps[bl], sig[i], ALU.mult, [f"ps{bl}", f"sig{i}"], [f"ag{i}"])
            DMA("sp", AGLU[c, :, PADC + t0:PADC + t0 + TT], ag[i], [f"ag{i}"], ["AGLU"])
        ckpt(3)
        n = 0
        for p in range(3):
            for part in range(2):
                for hc in range(4):
                    col = 1024 + (p * 3 + part) * 512 + hc * 128
                    b = bank()
                    for kc in range(8):
                        MM(ps[b], W[:, kc, col:col + 128], hT[:, kc, :], kc == 0, kc == 7, ["w_in"] + hk, [f"ps{b}"])
                    i = n % 2
                    n += 1
                    ACT(qb[i], ps[b], AF.Identity, [f"ps{b}"], [f"qb{i}"], scale=(0.125 if part == 0 else 1.0))
                    b2 = bank2()
                    MM(ps[b2], Pm_b, qb[i], True, True, ["cb", f"qb{i}"], [f"ps{b2}"])
                    TTOP("dve", t1[i], qb[i], cosT, ALU.mult, [f"qb{i}", "cosT"], [f"t1{i}"])
                    TTOP("dve", t2[i], ps[b2], sinT, ALU.mult, [f"ps{b2}", "sinT"], [f"t2{i}"])
                    TTOP("pool", qo[i], t1[i], t2[i], ALU.add, [f"t1{i}", f"t2{i}"], [f"qo{i}"])
                    DMA("sp", QKs[p, part, hc, :, PADK + t0:PADK + t0 + TT], qo[i], [f"qo{i}"], ["QKs"])
        ckpt(4)
        for s in range(4):
            for p in range(3):
                col = 1024 + (p * 3 + 2) * 512
                b = bank()
                for kc in range(8):
                    MM(ps[b], hT[:, kc, s * 128:(s + 1) * 128], W[:, kc, col:col + 512], kc == 0, kc == 7,
                       ["w_in"] + hk, [f"ps{b}"])
                CP(ev(), vst[:, s, p * 512:(p + 1) * 512], ps[b], [f"ps{b}"], ["vst"])
        ckpt(5)
        for p in range(3):
            DMA("sp", Vs[p, PADK + t0:PADK + t0 + TT, :].rearrange("(s i) c -> i s c", i=128),
                vst[:, :, p * 512:(p + 1) * 512], ["vst"], ["Vs"])
        ckpt(6)

    S.barrier()
    A.reset()
    HW = TT + 30
    at = [A.alloc(f"at{i}", [128, 4, HW], F32) for i in range(2)]
    accD = [A.alloc(f"accD{i}", [128, TT], F32) for i in range(2)]
    accP = [A.alloc(f"accP{i}", [128, TT], F32) for i in range(2)]
    acc = A.alloc("acc", [128, 4, TT], F32)
    cab = [A.alloc(f"cab{i}", [128, TT], BF16) for i in range(2)]
    csq = [A.alloc(f"csq{i}", [128, TT], BF16) for i in range(2)]
    mean = A.alloc("mean", [128, TT], F32)
    var = A.alloc("var", [128, TT], F32)
    lt1 = [A.alloc(f"lt1{i}", [128, TT], F32) for i in range(2)]
    aco = A.alloc("aco", [128, 4, TT], BF16)
    NDVE = 31
    def cv_load(tt):
        t0 = tt * TT
        DMA("sp", at[tt % 2], AGLU[:, :, PADC + t0 - 15:PADC + t0 + TT + 15].rearrange("c p t -> p c t"),
            ["AGLU"], [f"at{tt % 2}"])

    cv_load(0)
    for tt in range(NT):
        if tt + 1 < NT:
            cv_load(tt + 1)
        seg, j = tt // 4, tt % 4
        t0 = tt * TT
        a_ = at[tt % 2]
        ak = f"at{tt % 2}"
        if j == 0:
            TS("dve", a_[:, :, 0:15], a_[:, :, 0:15], fl[:, 0, seg:seg + 1], None, ALU.mult, None, [ak, "fl"], [ak])
        if j == 3:
            TS("dve", a_[:, :, TT + 15:TT + 30], a_[:, :, TT + 15:TT + 30], fl[:, 1, seg:seg + 1], None,
               ALU.mult, None, [ak, "fl"], [ak])
        for c in range(4):
            i = c % 2
            TS("dve", accD[i], a_[:, c, 0:TT], cwt[:, c, 0:1], cvec[:, 0, c:c + 1], ALU.mult, ALU.add,
               [ak, "cwt", "cvec"], [f"accD{i}"])
            for k in range(1, NDVE):
                STT("dve", accD[i], a_[:, c, k:k + TT], cwt[:, c, k:k + 1], accD[i], ALU.mult, ALU.add,
                    [ak, f"accD{i}"], [f"accD{i}"])
            CP("pool", acc[:, c, :], accD[i], [f"accD{i}"], [("acc", c)])
            ACT(cab[i], acc[:, c, :], AF.Identity, [("acc", c)], [f"cab{i}"])
            ACT(csq[i], acc[:, c, :], AF.Square, [("acc", c)], [f"csq{i}"])
            MM(ps[0], ones_b, cab[i], c == 0, c == 3, [f"cab{i}"], ["ps0"])
            MM(ps[1], ones_b, csq[i], c == 0, c == 3, [f"csq{i}"], ["ps1"])
        ACT(mean, ps[0], AF.Identity, ["ps0"], ["mean"], scale=1.0 / 512)
        TTOP("dve", var, mean, mean, ALU.mult, ["mean"], ["var"])
        STT("dve", var, ps[1], 1.0 / 512, var, ALU.mult, ALU.subtract, ["ps1", "var"], ["var"])
        ACT(var, var, AF.Sqrt, ["var"], ["var"], scale=1.0, bias=cx.eps_col)
        RECIP(var, var, ["var"], ["var"])
        for c in range(4):
            i = c % 2
            TTOP("dve", lt1[i], acc[:, c, :], mean, ALU.subtract, [("acc", c), "mean"], [f"lt1{i}"])
            TTOP("pool", lt1[i], lt1[i], var, ALU.mult, [f"lt1{i}", "var"], [f"lt1{i}"])
            ACT(aco[:, c, :], lt1[i], AF.Silu, [f"lt1{i}", "cvec"], ["aco"],
                scale=cvec[:, 1, c:c + 1], bias=cvec[:, 2, c:c + 1])
        DMA("sp", ACV[:, :, t0:t0 + TT].rearrange("c p t -> p c t"), aco, ["aco"], ["ACV"])

    S.barrier()
    A.reset()
    qt_ = [A.alloc(f"qt{i}", [128, SEG], BF16) for i in range(2)]
    kt_ = [A.alloc(f"kt{i}", [128, SEG + 128 * 16], BF16) for i in range(2)]
    vd = [A.alloc(f"vd{i}", [128, 32, 512], BF16) for i in range(2)]
    num_sb = A.alloc("num_sb", [128, 4, SEG], F32)
    den_sb = A.alloc("den_sb", [128, 4, SEG], F32)
    pt = [A.alloc(f"pt{i}", [128, 256], BF16) for i in range(4)]
    atto = [A.alloc(f"atto{i}", [128, SEG], BF16) for i in range(2)]
    sbank = Rot([0, 1, 2, 3])
    nbank = Rot([4, 5])
    dbank = Rot([6, 7])
    ptr = Rot([0, 1, 2, 3])
    ld = 0
    vl = 0
    for seg in range(NSEG):
        sb0 = seg * SEG
        for p, (win, d) in enumerate(PATTERNS):
            nqb = SEG // d // 128
            v_ = vd[vl % 2]
            vk = f"vd{vl % 2}"
            vl += 1
            for r in range(d):
                for m in range(nqb + 1):
                    start = PADK + sb0 - 64 * d + r + d * 128 * m
                    DMA("sp", v_[:, r * (nqb + 1) + m, :], Vs[p, dsl(start, d, 128), :], ["Vs"], [vk])
            for hc in range(4):
                q_ = qt_[ld % 2]
                k_ = kt_[ld % 2]
                qk, kk = f"qt{ld % 2}", f"kt{ld % 2}"
                ld += 1
                DMA("sp", q_, QKs[p, 0, hc, :, PADK + sb0:PADK + sb0 + SEG], ["QKs"], [qk])
                kw_ = SEG + 128 * d
                DMA("sp", k_[:, 0:kw_], QKs[p, 1, hc, :, PADK + sb0 - 64 * d:PADK + sb0 + SEG + 64 * d], ["QKs"], [kk])
                for r in range(d):
                    for j in range(nqb):
                        nb_, db_ = nbank(), dbank()
                        qa = q_[:, dsl(128 * d * j + r, d)]
                        for side in range(2):
                            m = j + side
                            ka = k_[:, dsl(128 * d * m + r, d)]
                            b = sbank()
                            pb = ps[b]
                            mk = maskA_b if side == 0 else maskB_b
                            MM(pb[:, 0:128], ident_b, mk[:, 0:128], True, False, ["cb"], [f"ps{b}"])
                            MM(pb[:, 0:128], ka[0:64, :], qa[0:64, :], False, True, [kk, qk], [f"ps{b}"])
                            MM(pb[:, 128:256], ident_b, mk[:, 0:128], True, False, ["cb"], [f"ps{b}"])
                            MM(pb[:, 128:256], ka[64:128, :], qa[64:128, :], False, True, [kk, qk], [f"ps{b}"])
                            pi = ptr()
                            if m == 0:
                                bias = fl[:, 2, seg:seg + 1]
                            elif m == nqb:
                                bias = fl[:, 3, seg:seg + 1]
                            else:
                                bias = 0.0
                            ACT(pt[pi], pb[:, 0:256], AF.Exp, [f"ps{b}", "fl"], [f"pt{pi}"], scale=1.0, bias=bias)
                            vt = v_[:, r * (nqb + 1) + m, hc * 128:(hc + 1) * 128]
                            MM(ps[nb_][0:64, 0:128], vt[:, 0:64], pt[pi][:, 0:128], side == 0, side == 1,
                               [vk, f"pt{pi}"], [f"ps{nb_}"])
                            MM(ps[nb_][64:128, 0:128], vt[:, 64:128], pt[pi][:, 128:256], side == 0, side == 1,
                               [vk, f"pt{pi}"], [f"ps{nb_}"])
                            MM(ps[db_][0:64, 0:128], ones_b[:, 0:64], pt[pi][:, 0:128], side == 0, side == 1,
                               ["cb", f"pt{pi}"], [f"ps{db_}"])
                            MM(ps[db_][64:128, 0:128], ones_b[:, 0:64], pt[pi][:, 128:256], side == 0, side == 1,
                               ["cb", f"pt{pi}"], [f"ps{db_}"])
                        na = num_sb[:, hc, dsl(128 * d * j + r, d)]
                        da = den_sb[:, hc, dsl(128 * d * j + r, d)]
                        if p == 0:
                            CP("dve", na, ps[nb_][:, 0:128], [f"ps{nb_}"], [("num", hc)])
                            CP("act", da, ps[db_][:, 0:128], [f"ps{db_}"], [("den", hc)])
                        else:
                            TTOP("dve", na, na, ps[nb_][:, 0:128], ALU.add, [f"ps{nb_}", ("num", hc)], [("num", hc)])
                            TTOP("dve", da, da, ps[db_][:, 0:128], ALU.add, [f"ps{db_}", ("den", hc)], [("den", hc)])
        for hc in range(4):
            ao = atto[hc % 2]
            RECIP(den_sb[:, hc, :], den_sb[:, hc, :], [("den", hc)], [("den", hc)])
            TTOP("pool", ao, num_sb[:, hc, :], den_sb[:, hc, :], ALU.mult, [("num", hc), ("den", hc)], [f"atto{hc % 2}"])
            DMA("sp", ATT[hc, :, sb0:sb0 + SEG], ao, [f"atto{hc % 2}"], ["ATT"])

    def phase_outproj(wdram, srcs, Gt):
        S.barrier()
        A.reset()
        Wo = A.alloc("w_o", [128, 8, D], BF16)
        load_weight(cx, Wo, wdram, 8, D, "w_o")
        inb = [A.alloc(f"inb{i}", [128, 8, TT], BF16) for i in range(2)]
        xr = [A.alloc(f"xr{i}", [128, 8, TT], F32) for i in range(2)]
        bk = Rot([0, 1, 2, 3])
        def op_loads(tt):
            t0 = tt * TT
            ib, xb_ = inb[tt % 2], xr[tt % 2]
            ik, xk = f"inb{tt % 2}", f"xr{tt % 2}"
            for (src, nchunk, c0, key) in srcs:
                DMA("sp", ib[:, c0:c0 + nchunk, :], src[:, :, t0:t0 + TT].rearrange("c p t -> p c t"), [key], [ik])
            DMA("sp", xb_, XT[:, :, t0:t0 + TT].rearrange("c p t -> p c t"), [("XT", tt)], [xk])

        op_loads(0)
        for tt in range(NT):
            if tt + 1 < NT:
                op_loads(tt + 1)
            seg = tt // 4
            t0 = tt * TT
            ib, xb_ = inb[tt % 2], xr[tt % 2]
            ik, xk = f"inb{tt % 2}", f"xr{tt % 2}"
            for m in range(8):
                b = bk()
                for kc in range(8):
                    MM(ps[b], Wo[:, kc, m * 128:(m + 1) * 128], ib[:, kc, :], kc == 0, kc == 7, ["w_o", ik], [f"ps{b}"])
                STT("dve", xb_[:, m, :], ps[b], Gt[:, m, seg:seg + 1], xb_[:, m, :], ALU.mult, ALU.add,
                    [f"ps{b}", xk, "modT"], [xk])
            DMA("sp", XT[:, :, t0:t0 + TT].rearrange("c p t -> p c t"), xb_, [xk], [("XT", tt)])

    def phase_mlp(l, final):
        S.barrier()
        A.reset()
        stg = [A.alloc(f"stg{i}", [128, 2048], F32) for i in range(2)]
        W1 = A.alloc("w1", [128, 8, 4 * D], BF16)
        load_weight(cx, W1, mlp_w1[l], 8, 4 * D, "w1", st=stg)
        W2 = A.alloc("w2", [128, 32, D], BF16)
        load_weight(cx, W2, mlp_w2[l], 32, D, "w2", stage_words=1024, st=stg)
        xr = [A.alloc("xr0", [128, 8, TT], F32)] * 2
        sq_ = [A.alloc(f"sq{i}", [128, TT], BF16) for i in range(2)]
        rstd_ = A.alloc("rstd", [128, TT], F32)
        tmp_ = [A.alloc(f"tmpn{i}", [128, TT], F32) for i in range(2)]
        h2 = A.alloc("hT", [128, 8, TT], BF16)
        hid = A.alloc("hid", [128, 8, TT], BF16)
        rl = [A.alloc(f"rl{i}", [128, TT], F32) for i in range(2)]
        yo = [stg[i].rearrange("p (a b) -> p a b", a=2) for i in range(2)]
        stk = ["stg_st0", "stg_st1"]
        bk = Rot([0, 1, 2, 3])
        bk2 = Rot([4, 5])
        e2 = Rot(["dve", "act"])
        for tt in range(NT):
            seg = tt // 4
            t0 = tt * TT
            xb_ = xr[tt % 2]
            xk = "xr0"
            DMA("sp", xb_, XT[:, :, t0:t0 + TT].rearrange("c p t -> p c t"), ["XT"], [xk])
            for c in range(8):
                s = sq_[c % 2]
                ACT(s, xb_[:, c, :], AF.Square, [xk], [f"sq{c % 2}"])
                MM(ps[6], ones_b, s, c == 0, c == 7, [f"sq{c % 2}"], ["ps6"])
            ACT(rstd_, ps[6], AF.Sqrt, ["ps6"], ["rstd"], scale=1.0 / 1024, bias=cx.eps_col)
            RECIP(rstd_, rstd_, ["rstd"], ["rstd"])
            for c in range(8):
                t = tmp_[c % 2]
                STT("dve", t, xb_[:, c, :], G2[l][:, c, seg:seg + 1], rstd_, ALU.mult, ALU.mult,
                    [xk, "rstd", "G"], [f"tmpn{c % 2}"])
                ACT(h2[:, c, :], t, AF.Identity, [f"tmpn{c % 2}", "modT"], [("hT", c)], scale=1.0,
                    bias=SH2[l][:, c, seg:seg + 1])
            hk = [("hT", c) for c in range(8)]
            for qtr in range(4):
                for f in range(8):
                    fc = qtr * 8 + f
                    b = bk()
                    for kc in range(8):
                        MM(ps[b], W1[:, kc, fc * 128:(fc + 1) * 128], h2[:, kc, :], kc == 0, kc == 7,
                           ["w1"] + hk, [f"ps{b}"])
                    i = f % 2
                    if i == 0:
                        ACT(rl[i], ps[b], AF.Relu, [f"ps{b}"], [f"rl{i}"])
                        TTOP("pool", hid[:, f, :], rl[i], rl[i], ALU.mult, [f"rl{i}"], [("hid", f)])
                    else:
                        TS("dve", rl[i], ps[b], 0.0, None, ALU.max, None, [f"ps{b}"], [f"rl{i}"])
                        ACT(hid[:, f, :], rl[i], AF.Square, [f"rl{i}"], [("hid", f)])
                hdk = [("hid", f) for f in range(8)]
                for m in range(8):
                    b = bk2()
                    for f in range(8):
                        fc = qtr * 8 + f
                        MM(ps[b], W2[:, fc, m * 128:(m + 1) * 128], hid[:, f, :], f == 0, f == 7,
                           ["w2"] + hdk, [f"ps{b}"])
                    STT("dve", xb_[:, m, :], ps[b], GG2[l][:, m, seg:seg + 1], xb_[:, m, :], ALU.mult, ALU.add,
                        [f"ps{b}", xk, "modT"], [xk])
            if not final:
                DMA("sp", XT[:, :, t0:t0 + TT].rearrange("c p t -> p c t"), xb_, [xk], ["XT"])
            else:
                for c in range(8):
                    s = sq_[c % 2]
                    ACT(s, xb_[:, c, :], AF.Square, [xk], [f"sq{c % 2}"])
                    MM(ps[6], ones_b, s, c == 0, c == 7, [f"sq{c % 2}"], ["ps6"])
                ACT(rstd_, ps[6], AF.Sqrt, ["ps6"], ["rstd"], scale=1.0 / 1024, bias=cx.eps_col)
                RECIP(rstd_, rstd_, ["rstd"], ["rstd"])
                for c in range(8):
                    STT("dve", xb_[:, c, :], xb_[:, c, :], nrm[:, 4, c:c + 1], rstd_, ALU.mult, ALU.mult,
                        [xk, "rstd", "nrm"], [xk])
                for s in range(4):
                    yv = yo[s // 2][:, s % 2, :]
                    for cc in range(2):
                        b = bk()
                        for c4 in range(4):
                            c = cc * 4 + c4
                            TR(ps[b][:, c4 * 128:(c4 + 1) * 128], xb_[:, c, s * 128:(s + 1) * 128], identf,
                               [xk, "cf"], [f"ps{b}"])
                        CP(e2(), yv[:, cc * 512:(cc + 1) * 512], ps[b], [f"ps{b}"], stk)
                for hy in range(2):
                    DMA("sp", y[t0 + hy * 256:t0 + (hy + 1) * 256, :].rearrange("(s p) d -> p s d", p=128),
                        yo[hy], stk, ["y"])

    phase_outproj(ab_w_out, [(ACV, 4, 0, "ACV"), (ATT, 4, 4, "ATT")], GG1[0])
    phase_mlp(0, False)

    S.barrier()
    A.reset()
    Wc = A.alloc("w_c", [128, 8, 4128], BF16)
    load_weight(cx, Wc, c_w_in, 8, 4128, "w_c")
    xr = [A.alloc(f"xr{i}", [128, 8, TT], F32) for i in range(2)]
    sq = [A.alloc(f"sq{i}", [128, TT], BF16) for i in range(2)]
    rstd = A.alloc("rstd", [128, TT], F32)
    tmpn = [A.alloc(f"tmpn{i}", [128, TT], F32) for i in range(2)]
    hT = A.alloc("hT", [128, 8, TT], BF16)
    fo = [A.alloc(f"fo{i}", [128, 8, TT], BF16) for i in range(3)]
    vst1 = A.alloc("vst1", [128, 4, D], BF16)
    gst = A.alloc("gst", [128, 4, 32], F32)
    bank = Rot([0, 1, 2, 3])
    ev = Rot(["act", "dve"])
    KSC = 128.0 ** -0.5
    def p5_load(tt):
        DMA("sp", xr[tt % 2], XT[:, :, tt * TT:(tt + 1) * TT].rearrange("c p t -> p c t"), ["XT"], [f"xr{tt % 2}"])

    p5_load(0)
    for tt in range(NT):
        if tt + 1 < NT:
            p5_load(tt + 1)
        seg = tt // 4
        t0 = tt * TT
        xb_ = xr[tt % 2]
        xk = f"xr{tt % 2}"
        for c in range(8):
            s = sq[c % 2]
            ACT(s, xb_[:, c, :], AF.Square, [xk], [f"sq{c % 2}"])
            MM(ps[6], ones_b, s, c == 0, c == 7, [f"sq{c % 2}"], ["ps6"])
        ACT(rstd, ps[6], AF.Sqrt, ["ps6"], ["rstd"], scale=1.0 / 1024, bias=cx.eps_col)
        RECIP(rstd, rstd, ["rstd"], ["rstd"])
        for c in range(8):
            t = tmpn[c % 2]
            STT("dve", t, xb_[:, c, :], G1[1][:, c, seg:seg + 1], rstd, ALU.mult, ALU.mult,
                [xk, "rstd", "G"], [f"tmpn{c % 2}"])
            ACT(hT[:, c, :], t, AF.Identity, [f"tmpn{c % 2}", "modT"], [("hT", c)], scale=1.0,
                bias=SH1[1][:, c, seg:seg + 1])
        hk = [("hT", c) for c in range(8)]
        for which, cbase in ((0, 0), (1, 1024), (2, 3072)):
            for h in range(8):
                b = bank()
                col = cbase + h * 128
                for kc in range(8):
                    MM(ps[b], Wc[:, kc, col:col + 128], hT[:, kc, :], kc == 0, kc == 7, ["w_c"] + hk, [f"ps{b}"])
                if which == 0:
                    CP(ev(), fo[0][:, h, :], ps[b], [f"ps{b}"], ["fo0"])
                elif which == 1:
                    ACT(fo[1][:, h, :], ps[b], AF.Identity, [f"ps{b}"], ["fo1"], scale=KSC)
                else:
                    ACT(fo[2][:, h, :], ps[b], AF.Sigmoid, [f"ps{b}"], ["fo2"])
        DMA("sp", QT1[:, :, t0:t0 + TT].rearrange("c p t -> p c t"), fo[0], ["fo0"], ["QT1"])
        DMA("sp", KT1[:, :, t0:t0 + TT].rearrange("c p t -> p c t"), fo[1], ["fo1"], ["KT1"])
        DMA("sp", OS1[:, :, t0:t0 + TT].rearrange("c p t -> p c t"), fo[2], ["fo2"], ["OS1"])
        for s in range(4):
            for half in range(2):
                b = bank()
                col = 2048 + half * 512
                for kc in range(8):
                    MM(ps[b], hT[:, kc, s * 128:(s + 1) * 128], Wc[:, kc, col:col + 512], kc == 0, kc == 7,
                       ["w_c"] + hk, [f"ps{b}"])
                CP(ev(), vst1[:, s, half * 512:(half + 1) * 512], ps[b], [f"ps{b}"], ["vst1"])
            for kc in range(8):
                MM(ps[7][:, s * 32:(s + 1) * 32], hT[:, kc, s * 128:(s + 1) * 128], Wc[:, kc, 4096:4128],
                   kc == 0, kc == 7, ["w_c"] + hk, ["ps7"])
            TTOP("dve", gst[:, s, :], ps[7][:, s * 32:(s + 1) * 32], gb, ALU.add, ["ps7", "gb"], ["gst"])
        for g0 in (8, 24):
            ACT(gst[:, :, g0:g0 + 8], gst[:, :, g0:g0 + 8], AF.Exp, ["gst"], ["gst"], scale=-1.0)
            ACT(gst[:, :, g0:g0 + 8], gst[:, :, g0:g0 + 8], AF.Ln, ["gst"], ["gst"], scale=1.0, bias=1.0)
            TS("dve", gst[:, :, g0:g0 + 8], gst[:, :, g0:g0 + 8], -1.0, None, ALU.mult, None, ["gst"], ["gst"])
        DMA("sp", VT1[t0:t0 + TT, :].rearrange("(s i) c -> i s c", i=128), vst1, ["vst1"], ["VT1"])
        DMA("sp", GS1[t0:t0 + TT, :].rearrange("(s i) c -> i s c", i=128), gst, ["gst"], ["GS1"])

    NG = T // TT

    def phase_scan(bwd):
        S.barrier()
        A.reset()
        qg = [A.alloc(f"qg{i}", [128, 8, TT], BF16) for i in range(2)]
        kg = [A.alloc(f"kg{i}", [128, 8, TT], BF16) for i in range(2)]
        vg = [A.alloc(f"vg{i}", [128, 4, 8, 256], BF16) for i in range(2)]
        gg = [A.alloc(f"gg{i}", [128, 4, 32], F32) for i in range(2)]
        CN = A.alloc("CN", [128, 8, 256], F32)
        CNb = A.alloc("CNb", [128, 8, 256], BF16)
        hst = [A.alloc(f"hst{i}", [128, 8, TT], F32) for i in range(2)]
        asb = A.alloc("asb", [128, 8], F32)
        wsb = A.alloc("wsb", [128, 8], F32)
        dcy = A.alloc("dcy", [128, 8], F32)
        eb = [A.alloc(f"eb{i}", [128, 128], F32) for i in range(8)]
        dtl = [A.alloc(f"dt{i}", [128, 128], F32) for i in range(8)]
        scb = [A.alloc(f"scb{i}", [128, 128], BF16) for i in range(8)]
        qtb = [A.alloc(f"qtb{i}", [128, 128], BF16) for i in range(8)]
        kwb = [A.alloc(f"kwb{i}", [128, 128], BF16) for i in range(8)]
        rr = [A.alloc(f"rr{i}", [128, 128], F32) for i in range(8)]
        if bwd:
            hfg = [A.alloc("hfg0", [128, 8, TT], F32)] * 2
            osg = [A.alloc("osg0", [128, 8, TT], BF16)] * 2
            sqb = A.alloc("sqb", [128, TT], BF16)
            rs = A.alloc("rs", [128, TT], F32)
            gto = [A.alloc("gto0", [128, 8, TT], BF16)] * 2
        MEMSET("dve", CN, 0.0, ["CN"])
        MEMSET("pool", CNb, 0.0, ["CNb"])
        for i in range(2):
            MEMSET("pool", vg[i][:, :, :, 128:256], 1.0, [f"vg{i}"])
        Tri = Ltri if bwd else Utri
        mskb = maskL_f if bwd else maskU_f
        igo, lfo = (16, 24) if bwd else (0, 8)
        psT = ps[7].bitcast(BF16)
        def sc_load(gi):
            g = NG - 1 - gi if bwd else gi
            t0 = g * TT
            bi = gi % 2
            DMA("sp", qg[bi], QT1[:, :, t0:t0 + TT].rearrange("c p t -> p c t"), ["QT1"], [f"qg{bi}"])
            DMA("sp", kg[bi], KT1[:, :, t0:t0 + TT].rearrange("c p t -> p c t"), ["KT1"], [f"kg{bi}"])
            for ch in range(4):
                DMA("sp", vg[bi][:, ch, :, 0:128],
                    VT1[t0 + ch * 128:t0 + (ch + 1) * 128, :].rearrange("i (h e) -> i h e", e=128),
                    ["VT1"], [f"vg{bi}"])
            DMA("sp", gg[bi], GS1[t0:t0 + TT, :].rearrange("(s i) c -> i s c", i=128), ["GS1"], [f"gg{bi}"])

        def sc_load_bwd(gi):
            g = NG - 1 - gi
            t0 = g * TT
            DMA("sp", hfg[0], HF[:, :, t0:t0 + TT].rearrange("c p t -> p c t"), ["HF"], ["hfg"])
            DMA("sp", osg[0], OS1[:, :, t0:t0 + TT].rearrange("c p t -> p c t"), ["OS1"], ["osg"])

        sc_load(0)
        for gi in range(NG):
            g = NG - 1 - gi if bwd else gi
            t0 = g * TT
            bi = gi % 2
            q_, k_, v_, g_, hs_ = qg[bi], kg[bi], vg[bi], gg[bi], hst[bi]
            qk_, kk_, vk_, gk_, hk_ = f"qg{bi}", f"kg{bi}", f"vg{bi}", f"gg{bi}", f"hst{bi}"
            if bwd:
                sc_load_bwd(gi)
            if gi + 1 < NG:
                sc_load(gi + 1)
            for ci in range(4):
                ch = 3 - ci if bwd else ci
                chunk = g * 4 + ch
                cs_ = slice(ch * 128, (ch + 1) * 128)
                lf = g_[:, ch, lfo:lfo + 8]
                ig = g_[:, ch, igo:igo + 8]
                sm = ps[7][:, 256:272]
                MM(sm[:, 0:8], Tri, lf, True, True, ["cf", gk_], [("B", 7)])
                MM(sm[:, 8:16], onesf, lf, True, True, ["cf", gk_], [("B", 7)])
                TTOP("dve", asb, ig, sm[:, 0:8], ALU.subtract, [gk_, ("B", 7)], ["asb"])
                TTOP("dve", wsb, asb, sm[:, 8:16], ALU.add, ["asb", ("B", 7)], ["wsb"])
                ACT(wsb, wsb, AF.Exp, ["wsb"], ["wsb"])
                ACT(dcy, sm[:, 8:16], AF.Exp, [("B", 7)], ["dcy"])
                for hh in range(2):
                    heads = range(hh * 4, hh * 4 + 4)
                    for h in heads:
                        u = h % 4
                        pa = ps[u // 2]
                        co = (u % 2) * 256
                        lfb = lf[:, h:h + 1].to_broadcast([128, 128])
                        MM(pa[:, co:co + 128], lfb, Tri, True, True, [gk_, "cf"], [("B", u // 2)])
                        MM(pa[:, co + 128:co + 256], lfb, Tri, True, False, [gk_, "cf"], [("B", u // 2)])
                        MM(pa[:, co + 128:co + 256], identf, mskb, False, True, ["cf"], [("B", u // 2)])
                        MM(ps[2][:, u * 128:(u + 1) * 128], k_[:, h, cs_], q_[:, h, cs_], True, True,
                           [kk_, qk_], [("B", 2)])
                        TR(psT[:, u * 128:(u + 1) * 128], k_[:, h, cs_], ident_b, [kk_, "cb"], [("B", 7)])
                    for h in heads:
                        u = h % 4
                        pa = ps[u // 2]
                        co = (u % 2) * 256
                        ACT(eb[h], pa[:, co:co + 128], AF.Exp, [("B", u // 2)], [f"eb{h}"])
                        ACT(dtl[h], pa[:, co + 128:co + 256], AF.Exp, [("B", u // 2), "asb"], [f"dt{h}"],
                            scale=1.0, bias=asb[:, h:h + 1])
                        ACT(kwb[h], psT[:, u * 128:(u + 1) * 128], AF.Identity, [("B", 7), "wsb"], [f"kwb{h}"],
                            scale=wsb[:, h:h + 1])
                    for h in heads:
                        u = h % 4
                        TTOP("dve", scb[h], ps[2][:, u * 128:(u + 1) * 128], dtl[h], ALU.mult,
                             [("B", 2), f"dt{h}"], [f"scb{h}"])
                        TTOP("pool", qtb[h], q_[:, h, cs_], eb[h], ALU.mult, [qk_, f"eb{h}"], [f"qtb{h}"])
                    for h in heads:
                        u = h % 4
                        pn = ps[3 + u // 2]
                        co = (u % 2) * 256
                        MM(pn[:, co:co + 128], v_[:, ch, h, 0:128], scb[h], True, False, [vk_, f"scb{h}"], [("B", 3 + u // 2)])
                        MM(pn[:, co:co + 128], CNb[:, h, 0:128], qtb[h], False, True, [("CNb", h), f"qtb{h}"], [("B", 3 + u // 2)])
                        MM(pn[:, co + 128:co + 256], ones_b, scb[h], True, False, ["cb", f"scb{h}"], [("B", 3 + u // 2)])
                        MM(pn[:, co + 128:co + 256], CNb[:, h, 128:256], qtb[h], False, True,
                           [("CNb", h), f"qtb{h}"], [("B", 3 + u // 2)])
                        pu = ps[5 + u // 2]
                        MM(pu[:, co:co + 256], kwb[h], v_[:, ch, h, :], True, True, [f"kwb{h}", vk_], [("B", 5 + u // 2)])
                    for h in heads:
                        u = h % 4
                        pn = ps[3 + u // 2]
                        co = (u % 2) * 256
                        ACT(rr[h], pn[:, co + 128:co + 256], AF.Abs, [("B", 3 + u // 2)], [f"rr{h}"])
                        TS("dve", rr[h], rr[h], 1.0, None, ALU.max, None, [f"rr{h}"], [f"rr{h}"])
                        RECIP(rr[h], rr[h], [f"rr{h}"], [f"rr{h}"])
                        TTOP("dve", hs_[:, h, cs_], pn[:, co:co + 128], rr[h], ALU.mult, [("B", 3 + u // 2), f"rr{h}"], [(hk_, h)])
                        pu = ps[5 + u // 2]
                        STT("dve", CN[:, h, :], CN[:, h, :], dcy[:, h:h + 1], pu[:, co:co + 256], ALU.mult, ALU.add,
                            [("B", 5 + u // 2), "dcy", ("CN", h)], [("CN", h)])
                segb = (chunk % 16 == 0) if bwd else (chunk % 16 == 15)
                seg = chunk // 16
                nseg = seg - 1 if bwd else seg + 1
                allCN = [("CN", h) for h in range(8)]
                if segb and 0 <= nseg < NSEG:
                    fcol = fl[:, 1, nseg:nseg + 1] if bwd else fl[:, 0, nseg:nseg + 1]
                    TS("dve", CN, CN, fcol, None, ALU.mult, None, allCN + ["fl"], allCN)
                for h in range(8):
                    CP("act" if h % 2 == 0 else "pool", CNb[:, h, :], CN[:, h, :], [("CN", h)], [("CNb", h)])
            if not bwd:
                DMA("sp", HF[:, :, t0:t0 + TT].rearrange("c p t -> p c t"), hs_, [(hk_, h) for h in range(8)], ["HF"])
            else:
                hf_, os_, go_ = hfg[bi], osg[bi], gto[bi]
                for h in range(8):
                    b = h % 2
                    bkeys = [("B", b)]
                    TTOP("dve", hs_[:, h, :], hs_[:, h, :], hf_[:, h, :], ALU.add, [(hk_, h), "hfg"], [(hk_, h)])
                    ACT(sqb, hs_[:, h, :], AF.Square, [(hk_, h)], ["sqb"])
                    MM(ps[b], ones_b, sqb, True, True, ["sqb", "cb"], bkeys)
                    ACT(rs, ps[b], AF.Sqrt, bkeys, ["rs"], scale=1.0 / 128, bias=cx.eps_col)
                    RECIP(rs, rs, ["rs"], ["rs"])
                    TTOP("dve", hs_[:, h, :], hs_[:, h, :], rs, ALU.mult, [(hk_, h), "rs"], [(hk_, h)])
                    STT("dve", go_[:, h, :], hs_[:, h, :], hn[:, h:h + 1], os_[:, h, :], ALU.mult, ALU.mult,
                        [(hk_, h), "hn", "osg"], ["gto"])
                DMA("sp", GT[:, :, t0:t0 + TT].rearrange("c p t -> p c t"), go_, ["gto"], ["GT"])

    phase_scan(False)
    phase_scan(True)
    phase_outproj(c_w_out, [(GT, 8, 0, "GT")], GG1[1])
    phase_mlp(1, True)


def make_consts():
    c = np.zeros((128, 128 * 8 + 512), np.float32)
    i = np.arange(128)
    o = 0
    c[:, o:o + 128] = np.eye(128); o += 128
    pm = np.zeros((128, 128), np.float32)
    for m in range(128):
        r = m % 64
        if r < 8:
            pm[m + 8, m] = 1.0
        elif r < 16:
            pm[m - 8, m] = 1.0
    c[:, o:o + 128] = pm; o += 128
    c[:, o:o + 128] = (i[:, None] <= i[None, :]); o += 128
    c[:, o:o + 128] = (i[:, None] >= i[None, :]); o += 128
    c[:, o:o + 128] = np.where(i[:, None] <= i[None, :], 0.0, NEG); o += 128
    c[:, o:o + 128] = np.where(i[:, None] >= i[None, :], 0.0, NEG); o += 128
    c[:, o:o + 128] = 1.0; o += 128
    c[:, o:o + 128] = 0.0; o += 128
    ma = np.where(i[:, None] >= i[None, :], 0.0, NEG)
    mb = np.where(i[:, None] <= i[None, :], 0.0, NEG)
    c[:, o:o + 256] = np.concatenate([ma, ma], 1); o += 256
    c[:, o:o + 256] = np.concatenate([mb, mb], 1); o += 256
    return c


def rope_tables(pos):
    half = 8
    inv = np.power(np.float32(500000.0), -np.arange(half, dtype=np.float32) / half).astype(np.float32)
    ang = pos.astype(np.float32)[None, :] * inv[:, None]
    cos = np.cos(ang).astype(np.float32)
    sin = np.sin(ang).astype(np.float32)
    Tn = pos.shape[0]
    ct = np.ones((128, Tn), np.float32)
    st = np.zeros((128, Tn), np.float32)
    for hb in (0, 64):
        ct[hb:hb + 8] = cos
        ct[hb + 8:hb + 16] = cos
        st[hb:hb + 8] = -sin
        st[hb + 8:hb + 16] = sin
    return ct, st


def core_inputs(xseg, cseg, pos, lo, hi, w):
    nseg = cseg.shape[0]
    f32 = np.float32
    m = {}
    m["x"] = np.ascontiguousarray(xseg, f32)
    m["cT"] = np.ascontiguousarray(cseg.T.reshape(8, 128, nseg).transpose(1, 0, 2), f32)
    m["ropec"], m["ropes"] = rope_tables(pos)
    fl = np.zeros((128, 4, nseg), f32)
    fl[:, 0, :] = lo[None, :]
    fl[:, 1, :] = hi[None, :]
    k = np.arange(128)
    fl[:, 2, :] = np.where((k[:, None] < 64) & (lo[None, :] == 0), NEG, 0.0)
    fl[:, 3, :] = np.where((k[:, None] >= 64) & (hi[None, :] == 0), NEG, 0.0)
    m["flags"] = fl
    m.update(w)
    return m


def weight_inputs(inp):
    f32 = np.float32
    w = {}
    w["ada_w"] = np.ascontiguousarray(inp["ada_w"], f32)
    w["ada_bT"] = np.ascontiguousarray(np.asarray(inp["ada_b"], f32).reshape(2, 48, 128).transpose(2, 0, 1))
    nr = np.stack([inp["norm_mix"][0], inp["norm_mix"][1], inp["norm_mlp"][0], inp["norm_mlp"][1],
                   inp["norm_final"]], 0).astype(f32)
    w["norms"] = np.ascontiguousarray(nr.reshape(5, 8, 128).transpose(2, 0, 1))
    w["ab_w_in"] = np.ascontiguousarray(inp["ab_w_in"][0], f32)
    w["conv_wT"] = np.ascontiguousarray(np.asarray(inp["conv_w"][0], f32).T.reshape(4, 128, 31).transpose(1, 0, 2))
    cv = np.stack([inp["conv_b"][0], inp["conv_ln_g"][0], inp["conv_ln_b"][0]], 0).astype(f32)
    w["conv_vec"] = np.ascontiguousarray(cv.reshape(3, 4, 128).transpose(2, 0, 1))
    w["ab_w_out"] = np.ascontiguousarray(inp["ab_w_out"][0], f32)
    w["c_w_in"] = np.ascontiguousarray(inp["c_w_in"][0], f32)
    w["gate_b"] = np.ascontiguousarray(np.broadcast_to(np.asarray(inp["c_gate_b"][0], f32)[None, :], (128, 32)))
    w["head_norm"] = np.ascontiguousarray(np.asarray(inp["c_head_norm"][0], f32).reshape(8, 128).T)
    w["c_w_out"] = np.ascontiguousarray(inp["c_w_out"][0], f32)
    w["mlp_w1"] = np.ascontiguousarray(inp["mlp_w1"], f32)
    w["mlp_w2"] = np.ascontiguousarray(inp["mlp_w2"], f32)
    w["consts"] = make_consts()
    return w


_CACHE = {}


def kernel(**inputs):
    inp = {k: np.asarray(v) for k, v in inputs.items()}
    NSEG = 8
    xp, xs_ = inp["x_prompt"], inp["x_sample"]
    cp, cs_ = inp["c_prompt"], inp["c_sample"]
    w = weight_inputs(inp)
    in_maps = []
    T = NSEG * SEG
    for b in range(2):
        lo = np.ones(NSEG, np.float32); lo[0] = 0
        hi = np.ones(NSEG, np.float32); hi[-1] = 0
        in_maps.append(core_inputs(xp[b], np.broadcast_to(cp[b][None, :], (NSEG, D)), np.arange(T), lo, hi, w))
    for c in range(6):
        cc = c % 4
        sl = slice(cc * 8, cc * 8 + 8)
        z = np.zeros(NSEG, np.float32)
        in_maps.append(core_inputs(xs_[sl].reshape(T, D), cs_[sl], np.tile(np.arange(SEG), NSEG), z, z, w))
    if "nc" not in _CACHE:
        _CACHE["nc"] = build(NSEG)[0]
    nc = _CACHE["nc"]
    res = run_bass_kernel_spmd(nc, in_maps, core_ids=list(range(8)))
    yp = np.stack([res.results[b]["y"].reshape(16384, D) for b in range(2)], 0).astype(np.float32)
    ys = np.concatenate([res.results[2 + c]["y"].reshape(8, SEG, D) for c in range(4)], 0).astype(np.float32)
    return (yp, ys)
```
